# Optimizing a Trainium2 kernel written in Bass

```python
import jax
import jax.numpy as jnp
from jax import lax
import numpy as np

D_MODEL = 2048
BATCH = 4
SEQ = 8192
DEPTH = 1
DEC_BATCH = 8
DEC_SEQ = 2048
PAST_LEN = 128

HEAD_DIM = 128
N_HEADS_A = D_MODEL // 256
N_KV_A = max(1, N_HEADS_A // 4)
GROUP_A = N_HEADS_A // N_KV_A
N_HEADS_B = D_MODEL // 256
N_KV_B = max(1, N_HEADS_B // 4)
GROUP_B = N_HEADS_B // N_KV_B
Q_WIDTH_A = N_HEADS_A * HEAD_DIM
KV_WIDTH_A = N_KV_A * HEAD_DIM
Q_WIDTH_B = N_HEADS_B * HEAD_DIM
KV_WIDTH_B = N_KV_B * HEAD_DIM
IN_COLS = Q_WIDTH_A + 2 * KV_WIDTH_A + Q_WIDTH_B + 2 * KV_WIDTH_B + 2 * D_MODEL
D_FF = ((8 * D_MODEL // 3 + 255) // 256) * 256
CONV_WIDTH = 3
BLOCK_Q = 128
WINDOW = 128
GRID_W = 64
ROPE_THETA = 10000.0
EPS = 1e-6

kernel_name = 'hybrid_gated_axial_window_encoder'


def rms_norm(x, gain):
    xf = x.astype(jnp.float32)
    xf = xf * lax.rsqrt(jnp.mean(xf * xf, axis=-1, keepdims=True) + EPS)
    return (xf * gain.astype(jnp.float32)).astype(x.dtype)


def rope_cos_sin(pos, dim):
    inv_freq = ROPE_THETA ** (-jnp.arange(0, dim, 2, dtype=jnp.float32) / dim)
    ang = pos.astype(jnp.float32)[:, None] * inv_freq[None, :]
    ang = jnp.concatenate([ang, ang], axis=-1)
    return jnp.cos(ang), jnp.sin(ang)


def apply_rope(x, cos, sin):
    half = x.shape[-1] // 2
    xf = x.astype(jnp.float32)
    rot = jnp.concatenate([-xf[..., half:], xf[..., :half]], axis=-1)
    return (xf * cos[None, :, None, :] + rot * sin[None, :, None, :]).astype(x.dtype)


def apply_axial_rope(x, cos_r, sin_r, cos_c, sin_c):
    h = HEAD_DIM // 2
    return jnp.concatenate([apply_rope(x[..., :h], cos_r, sin_r),
                            apply_rope(x[..., h:], cos_c, sin_c)], axis=-1)


def dense_attention(q, k, v):
    B, S, Hkv, G, Dh = q.shape
    nb = S // BLOCK_Q
    qb = q.reshape(B, nb, BLOCK_Q, Hkv, G, Dh).transpose(1, 0, 2, 3, 4, 5)
    scale = Dh ** -0.5

    def one_block(qi):
        s = jnp.einsum('bqhgd,bkhd->bhgqk', qi, k, preferred_element_type=jnp.float32) * scale
        p = jax.nn.softmax(s, axis=-1)
        return jnp.einsum('bhgqk,bkhd->bqhgd', p.astype(v.dtype), v)

    o = lax.map(one_block, qb)
    return o.transpose(1, 0, 2, 3, 4, 5).reshape(B, S, Hkv * G * Dh)


def windowed_attention(q, k, v, sink):
    B, S, Hkv, G, Dh = q.shape
    nb = S // BLOCK_Q
    span = BLOCK_Q + 2 * WINDOW
    kp = jnp.pad(k, ((0, 0), (WINDOW, WINDOW), (0, 0), (0, 0)))
    vp = jnp.pad(v, ((0, 0), (WINDOW, WINDOW), (0, 0), (0, 0)))
    a = jnp.arange(BLOCK_Q)[:, None]
    b = jnp.arange(span)[None, :]
    band = (b >= a) & (b <= a + 2 * WINDOW)
    qb = q.reshape(B, nb, BLOCK_Q, Hkv, G, Dh).transpose(1, 0, 2, 3, 4, 5)
    sink_l = sink.astype(jnp.float32).reshape(Hkv, G)[None, :, :, None, None]
    scale = Dh ** -0.5

    def one_block(args):
        i, qi = args
        start = i * BLOCK_Q
        ki = lax.dynamic_slice_in_dim(kp, start, span, axis=1)
        vi = lax.dynamic_slice_in_dim(vp, start, span, axis=1)
        key_pos = start - WINDOW + jnp.arange(span)
        valid = band & ((key_pos >= 0) & (key_pos < S))[None, :]
        s = jnp.einsum('bqhgd,bkhd->bhgqk', qi, ki, preferred_element_type=jnp.float32) * scale
        s = jnp.where(valid, s, -jnp.inf)
        m = jnp.maximum(jnp.max(s, axis=-1, keepdims=True), sink_l)
        p = jnp.exp(s - m)
        p = p / (jnp.sum(p, axis=-1, keepdims=True) + jnp.exp(sink_l - m))
        return jnp.einsum('bhgqk,bkhd->bqhgd', p.astype(vi.dtype), vi)

    o = lax.map(one_block, (jnp.arange(nb), qb))
    return o.transpose(1, 0, 2, 3, 4, 5).reshape(B, S, Hkv * G * Dh)


def token_mixer(h, w_in, q_norm_a, k_norm_a, sink_b, w_branch_a, w_branch_b, w_out):
    B, S, _ = h.shape
    rows = S // GRID_W
    t = jnp.arange(S, dtype=jnp.int32)
    row_pos = jnp.repeat(jnp.arange(rows, dtype=jnp.int32), GRID_W)
    col_pos = jnp.tile(jnp.arange(GRID_W, dtype=jnp.int32), rows)
    cos_r, sin_r = rope_cos_sin(row_pos, HEAD_DIM // 2)
    cos_c, sin_c = rope_cos_sin(col_pos, HEAD_DIM // 2)
    cos_t, sin_t = rope_cos_sin(t, HEAD_DIM)

    proj = h @ w_in
    sizes = (Q_WIDTH_A, KV_WIDTH_A, KV_WIDTH_A, Q_WIDTH_B, KV_WIDTH_B, KV_WIDTH_B, D_MODEL, D_MODEL)
    offsets = [sum(sizes[:i]) for i in range(1, len(sizes))]
    qa, ka, va, qb, kb, vb, gate_a, gate_b = jnp.split(proj, offsets, axis=-1)

    qa = rms_norm(qa.reshape(B, S, N_HEADS_A, HEAD_DIM), q_norm_a)
    ka = rms_norm(ka.reshape(B, S, N_KV_A, HEAD_DIM), k_norm_a)
    qa = apply_axial_rope(qa, cos_r, sin_r, cos_c, sin_c)
    ka = apply_axial_rope(ka, cos_r, sin_r, cos_c, sin_c)
    va = va.reshape(B, S, N_KV_A, HEAD_DIM)
    o_a = dense_attention(qa.reshape(B, S, N_KV_A, GROUP_A, HEAD_DIM), ka, va)

    qb = apply_rope(qb.reshape(B, S, N_HEADS_B, HEAD_DIM), cos_t, sin_t)
    kb = apply_rope(kb.reshape(B, S, N_KV_B, HEAD_DIM), cos_t, sin_t)
    vb = vb.reshape(B, S, N_KV_B, HEAD_DIM)
    o_b = windowed_attention(qb.reshape(B, S, N_KV_B, GROUP_B, HEAD_DIM), kb, vb, sink_b)

    merged = jax.nn.sigmoid(gate_a) * (o_a @ w_branch_a) + jax.nn.sigmoid(gate_b) * (o_b @ w_branch_b)
    return merged @ w_out


def channel_mixer(h, w_up, conv_w, conv_b, w_down):
    S = h.shape[1]
    up = h @ w_up
    pad = CONV_WIDTH // 2
    up_p = jnp.pad(up, ((0, 0), (pad, pad), (0, 0)))
    conv = conv_b
    for j in range(CONV_WIDTH):
        conv = conv + up_p[:, j:j + S] * conv_w[j]
    a, b = jnp.split(conv, 2, axis=-1)
    return (jax.nn.gelu(a, approximate=True) * b) @ w_down


def run_trunk(x, norm_pre_mix, w_in, q_norm_a, k_norm_a, sink_b, w_branch_a, w_branch_b, w_out,
              norm_post_mix, norm_pre_ffn, w_up, conv_w, conv_b, w_down, norm_post_ffn):
    for l in range(DEPTH):
        h = rms_norm(x, norm_pre_mix[l])
        mix = token_mixer(h, w_in[l], q_norm_a[l], k_norm_a[l], sink_b[l],
                          w_branch_a[l], w_branch_b[l], w_out[l])
        x = x + rms_norm(mix, norm_post_mix[l])
        h = rms_norm(x, norm_pre_ffn[l])
        ffn = channel_mixer(h, w_up[l], conv_w[l], conv_b[l], w_down[l])
        x = x + rms_norm(ffn, norm_post_ffn[l])
    return x


def setup_inputs(seed: int = 0) -> dict:
    key = jax.random.key(seed)
    ks = jax.random.split(key, 17)
    f32 = jnp.float32

    def dense(k, shape, fan_in):
        return jax.random.normal(k, shape, f32) * fan_in ** -0.5

    def gain(k, n):
        return 1.0 + 0.05 * jax.random.normal(k, (DEPTH, n), f32)

    return {
        'x_prompt': jax.random.normal(ks[0], (BATCH, SEQ, D_MODEL), f32),
        'x_sample': jax.random.normal(ks[1], (DEC_BATCH, DEC_SEQ, D_MODEL), f32),
        'norm_pre_mix': gain(ks[2], D_MODEL),
        'w_in': dense(ks[3], (DEPTH, D_MODEL, IN_COLS), D_MODEL),
        'q_norm_a': gain(ks[4], HEAD_DIM),
        'k_norm_a': gain(ks[5], HEAD_DIM),
        'sink_b': 0.5 * jax.random.normal(ks[6], (DEPTH, N_HEADS_B), f32),
        'w_branch_a': dense(ks[7], (DEPTH, Q_WIDTH_A, D_MODEL), Q_WIDTH_A),
        'w_branch_b': dense(ks[8], (DEPTH, Q_WIDTH_B, D_MODEL), Q_WIDTH_B),
        'w_out': dense(ks[9], (DEPTH, D_MODEL, D_MODEL), D_MODEL),
        'norm_post_mix': gain(ks[10], D_MODEL),
        'norm_pre_ffn': gain(ks[11], D_MODEL),
        'w_up': dense(ks[12], (DEPTH, D_MODEL, 2 * D_FF), D_MODEL),
        'conv_w': dense(ks[13], (DEPTH, CONV_WIDTH, 2 * D_FF), CONV_WIDTH),
        'conv_b': 0.02 * jax.random.normal(ks[14], (DEPTH, 2 * D_FF), f32),
        'w_down': dense(ks[15], (DEPTH, D_FF, D_MODEL), D_FF),
        'norm_post_ffn': gain(ks[16], D_MODEL),
    }


def reference(x_prompt, x_sample, norm_pre_mix, w_in, q_norm_a, k_norm_a, sink_b, w_branch_a,
              w_branch_b, w_out, norm_post_mix, norm_pre_ffn, w_up, conv_w, conv_b, w_down,
              norm_post_ffn):
    y_prompt = run_trunk(x_prompt, norm_pre_mix, w_in, q_norm_a, k_norm_a, sink_b, w_branch_a,
                         w_branch_b, w_out, norm_post_mix, norm_pre_ffn, w_up, conv_w, conv_b,
                         w_down, norm_post_ffn)
    y_sample = run_trunk(x_sample, norm_pre_mix, w_in, q_norm_a, k_norm_a, sink_b, w_branch_a,
                         w_branch_b, w_out, norm_post_mix, norm_pre_ffn, w_up, conv_w, conv_b,
                         w_down, norm_post_ffn)
    return (y_prompt, y_sample)
```

```python
import contextlib
import numpy as np
import ml_dtypes
import concourse.bass as bass
import concourse.mybir as mybir
from concourse.bass_utils import run_bass_kernel_spmd

F32 = mybir.dt.float32
BF16 = mybir.dt.bfloat16
AF = mybir.ActivationFunctionType
ALU = mybir.AluOpType

D = 2048
NKC = 16
DFF = 5632
NFC = 44
HD = 128
EPS = 1e-6
THETA = 10000.0
GRID_W = 64
SCALE = HD ** -0.5
DSPLIT = 352


class Buf:
    __slots__ = ("name", "w", "r", "excl")

    def __init__(self, name="", excl=False):
        self.name = name
        self.w = {}
        self.r = {}
        self.excl = excl


class _Rec:
    def __init__(self):
        self.call = None

    def __getattr__(self, name):
        def f(*a, **k):
            self.call = (name, a, k)
            return self
        return f


class Prog:
    ENGS = ("pe", "act", "dve", "pool", "sp")

    def __init__(self, nc, stack, n_dma_sems=32):
        self.nc = nc
        self.ops = {e: [] for e in self.ENGS}
        self.cnt = {e: 0 for e in self.ENGS}
        self.seen = {e: {} for e in self.ENGS}
        self.n_dma_sems = n_dma_sems
        self.dma_cnt = [0] * n_dma_sems
        self.n_sw = 8
        self.dma_rr = {"sw": 0, "hw": 0}
        self.esem = {e: stack.enter_context(nc.semaphore("s_" + e)) for e in self.ENGS}
        self.dsem = [stack.enter_context(nc.semaphore("d%d" % i)) for i in range(n_dma_sems)]
        self.n_ops = 0
        self.n_wait = 0

    def _needs(self, reads, writes):
        need = {}
        reads = [b for b in reads if b is not None]
        writes = [b for b in writes if b is not None]
        for b in reads:
            for k, v in b.w.items():
                if v > need.get(k, 0):
                    need[k] = v
        for b in writes:
            for k, v in b.w.items():
                if v > need.get(k, 0):
                    need[k] = v
            for k, v in b.r.items():
                if v > need.get(k, 0):
                    need[k] = v
        return need

    def _emit_waits(self, eng, need, skip_self=False):
        seen = self.seen[eng]
        for k, v in need.items():
            if skip_self and k == eng:
                continue
            if seen.get(k, 0) >= v:
                continue
            if isinstance(k, str):
                assert v <= self.cnt[k], ("wait on not-yet-signalled op", eng, k, v, self.cnt[k])
            seen[k] = v
            self.ops[eng].append(("wait", k, v))
            self.n_wait += 1

    def _mark(self, key, val, reads, writes, more=()):
        kvs = [(key, val)] + list(more)
        reads = [b for b in reads if b is not None]
        writes = [b for b in writes if b is not None]
        for b in reads:
            for k, v in kvs:
                if b.r.get(k, 0) < v:
                    b.r[k] = v
        for b in writes:
            b.w = dict(kvs)
            b.r = {}

    def op(self, eng, fn, reads=(), writes=(), signal=True):
        skip_self = (eng == "pe")
        if any(b is not None and b.excl for b in reads):
            writes = list(writes) + [b for b in reads if b is not None and b.excl]
            reads = [b for b in reads if b is None or not b.excl]
        need = self._needs(reads, writes)
        self._emit_waits(eng, need, skip_self=skip_self)
        if signal:
            self.cnt[eng] += 1
            val = self.cnt[eng]
        else:
            val = self.cnt[eng] + 1
        rec = _Rec()
        fn(rec)
        self.ops[eng].append(("op", rec.call, signal))
        self._mark(eng, val, reads, writes)
        self.n_ops += 1

    def dma(self, q, out, in_, reads=(), writes=()):
        need = self._needs(reads, writes)
        if q == "pool":
            i = self.dma_rr["sw"]
            self.dma_rr["sw"] = (i + 1) % self.n_sw
        else:
            i = self.n_sw + self.dma_rr["hw"]
            self.dma_rr["hw"] = (self.dma_rr["hw"] + 1) % (self.n_dma_sems - self.n_sw)
        key = ("d", i)
        if self.dma_cnt[i] > 0:
            need[key] = max(need.get(key, 0), self.dma_cnt[i])
        self._emit_waits(q, need)
        self.dma_cnt[i] += 16
        self.ops[q].append(("dma", out, in_, i))
        self._mark(key, self.dma_cnt[i], reads, writes)
        self.n_ops += 1

    def dma_multi(self, q, pieces, reads=(), writes=()):
        need = self._needs(reads, writes)
        kvs = []
        for (out, in_) in pieces:
            if q == "pool":
                i = self.dma_rr["sw"]
                self.dma_rr["sw"] = (i + 1) % self.n_sw
            else:
                i = self.n_sw + self.dma_rr["hw"]
                self.dma_rr["hw"] = (self.dma_rr["hw"] + 1) % (self.n_dma_sems - self.n_sw)
            key = ("d", i)
            if self.dma_cnt[i] > 0:
                need[key] = max(need.get(key, 0), self.dma_cnt[i])
            self._emit_waits(q, need)
            need = {}
            self.dma_cnt[i] += 16
            self.ops[q].append(("dma", out, in_, i))
            kvs.append((key, self.dma_cnt[i]))
            self.n_ops += 1
        self._mark(kvs[0][0], kvs[0][1], reads, writes, more=kvs[1:])

    def barrier(self):
        for e in self.ENGS:
            need = {}
            for k in self.ENGS:
                if k != e and self.cnt[k] > 0:
                    need[k] = self.cnt[k]
            for i in range(self.n_dma_sems):
                if self.dma_cnt[i] > 0:
                    need[("d", i)] = self.dma_cnt[i]
            self._emit_waits(e, need)

    def emit(self):
        nc = self.nc

        def semof(k):
            return self.esem[k] if isinstance(k, str) else self.dsem[k[1]]

        with nc.Block() as block:
            def run(engname):
                ops = self.ops[engname]

                def body(eng):
                    for o in ops:
                        if o[0] == "wait":
                            eng.wait_ge(semof(o[1]), o[2])
                        elif o[0] == "op":
                            ins = getattr(eng, o[1][0])(*o[1][1], **o[1][2])
                            if o[2]:
                                ins.then_inc(self.esem[engname], 1)
                        else:
                            eng.dma_start(out=o[1], in_=o[2]).then_inc(self.dsem[o[3]], 16)
                return body

            block.tensor(run("pe"))
            block.scalar(run("act"))
            block.vector(run("dve"))
            block.gpsimd(run("pool"))
            block.sync(run("sp"))
        self.ops = {e: [] for e in self.ENGS}


_UID = [0]


def _uname(name):
    _UID[0] += 1
    return "%s_u%d" % (name, _UID[0])


class Ring:
    def __init__(self, stack, nc, name, shape, dt, n, psum=False):
        self.items = []
        for i in range(n):
            if psum:
                t = stack.enter_context(nc.psum_tensor(_uname(name), shape, dt))
            else:
                t = stack.enter_context(nc.sbuf_tensor(_uname(name), shape, dt))
            self.items.append((t, Buf("%s%d" % (name, i), excl=psum)))
        self.i = 0

    def next(self):
        it = self.items[self.i]
        self.i = (self.i + 1) % len(self.items)
        return it


STOP_AFTER = [99]
import os as _os
DBG = _os.environ.get("KDBG", "")


def build_program(jobs):
    nc = bass.Bass("TRN2", target_bir_lowering=False)

    def din(name, shape, dt=F32):
        return nc.dram_tensor(name, shape, dt, kind="ExternalInput").ap()

    def dscr(name, shape, dt):
        return nc.dram_tensor(name, shape, dt, kind="Internal").ap()

    w_in = din("w_in", [D, 7168])
    w_ba = din("w_ba", [1024, D])
    w_bb = din("w_bb", [1024, D])
    w_out = din("w_out", [D, D])
    w_up = din("w_up", [D, 2 * DFF])
    w_down = din("w_down", [DFF, D])
    gbc_d = din("gbc", [4, 128, D])
    qkn_d = din("qkn", [128, 2])
    sink_d = din("sinkbc", [128, 8])
    convb_d = din("convb", [128, 88])
    identf_d = din("identf", [128, 128])
    cbf_d = din("cbf", [128, 1536], BF16)

    wsrc = [(w_in, [D, 7168]), (w_ba, [1024, D]), (w_bb, [1024, D]), (w_out, [D, D]), (w_up, [D, 2 * DFF]), (w_down, [DFF, D])]
    wbf = [dscr("wbf%d" % i, shp, BF16) for i, (_, shp) in enumerate(wsrc)]
    w_in_v, w_ba_v, w_bb_v, w_out_v, w_up_v, w_down_v = [w.rearrange("(kc p) c -> p kc c", p=128) for w in wbf]

    J = []
    for jb in jobs:
        n = jb["name"]
        S, NQ, NOUT = jb["S"], jb["NQ"], jb["NOUT"]
        NQP = ((NQ + 511) // 512) * 512
        j = dict(jb)
        j["NQP"] = NQP
        j["x"] = din("x_" + n, [S, D])
        j["tab"] = din("tab_" + n, [4, 128, S])
        j["convw"] = din("convw_" + n, [128, 3, 88])
        j["y"] = nc.dram_tensor("y_" + n, [NOUT, D], F32, kind="ExternalOutput").ap()
        j["KAT"] = dscr("KAT_" + n, [2, 128, S], BF16)
        j["KBT"] = dscr("KBT_" + n, [2, 128, S], BF16)
        j["VA"] = dscr("VA_" + n, [S, 256], BF16)
        j["VB"] = dscr("VB_" + n, [S, 256], BF16)
        j["QAT"] = dscr("QAT_" + n, [8, 128, NQP], BF16)
        j["QBT"] = dscr("QBT_" + n, [8, 128, NQP], BF16)
        j["OAT"] = dscr("OAT_" + n, [8, 128, NQP], BF16)
        j["OBT"] = dscr("OBT_" + n, [8, 128, NQP], BF16)
        j["X1"] = dscr("X1_" + n, [NQP, D], F32)
        j["H2T"] = dscr("H2T_" + n, [16, 128, NQP], BF16)
        j["B"] = {k: None for k in ("KAT", "KBT", "VA", "VB", "QAT", "QBT", "OAT", "OBT", "X1", "H2T")}
        J.append(j)
    SMAX = max(j["S"] for j in J)

    top = contextlib.ExitStack()
    with top:
        P = Prog(nc, top)

        def sb(stack, name, shape, dt=F32):
            return stack.enter_context(nc.sbuf_tensor(_uname(name), shape, dt))

        identf = sb(top, "identf", [128, 128]); B_identf = Buf()
        cbf = sb(top, "cbf", [128, 1536], BF16); B_cbf = Buf()
        onesf = sb(top, "onesf", [128, 128]); B_onesf = Buf()
        qkn = sb(top, "qkn", [128, 2]); B_qkn = Buf()
        epsb = sb(top, "epsb", [128, 1]); B_epsb = Buf()
        expsink = sb(top, "expsink", [128, 8]); B_expsink = Buf()
        convb = sb(top, "convb", [128, 88]); B_convb = Buf()
        identb = cbf[:, 0:128]
        onesb = cbf[:, 128:256]
        rotA = cbf[:, 256:384]
        rotB = cbf[:, 384:512]
        maskP = cbf[:, 512:1024]
        maskN = cbf[:, 1024:1536]
        P.dma("sp", identf[:], identf_d[:, :], writes=[B_identf])
        P.dma("sp", cbf[:], cbf_d[:, :], writes=[B_cbf])
        P.dma("sp", qkn[:], qkn_d[:, :], writes=[B_qkn])
        P.dma("sp", expsink[:], sink_d[:, :], writes=[B_expsink])
        P.dma("sp", convb[:], convb_d[:, :], writes=[B_convb])
        P.op("dve", lambda e: e.memset(onesf[:], 1.0), writes=[B_onesf])
        P.op("dve", lambda e: e.memset(epsb[:], EPS), writes=[B_epsb])
        P.op("act", lambda e: e.activation(out=expsink[:], in_=expsink[:], func=AF.Exp),
             reads=[B_expsink], writes=[B_expsink])

        late_casts = []
        for wi, ((wf, (K_, C_)), wb_) in enumerate(zip(wsrc, wbf)):
            for r0 in range(0, K_, 128):
                for c0 in range(0, C_, 2048):
                    cw_ = min(2048, C_ - c0)
                    if wi == 0:
                        P.dma("pool", wb_[r0:r0 + 128, c0:c0 + cw_], wf[r0:r0 + 128, c0:c0 + cw_])
                    else:
                        late_casts.append((wb_[r0:r0 + 128, c0:c0 + cw_], wf[r0:r0 + 128, c0:c0 + cw_]))
        P.barrier()

        def issue_late_casts(n):
            for _ in range(n):
                if late_casts:
                    d_, s_ = late_casts.pop(0)
                    P.dma("pool", d_, s_)
        if DBG == "T1":
            P.emit()
            return nc

        def mm(out, lhsT, rhs, start, stop, reads, writes, signal):
            P.op("pe", lambda e: e.matmul(out, lhsT=lhsT, rhs=rhs, start=start, stop=stop),
                 reads=reads, writes=writes, signal=signal)

        def mm_group(out, pairs, reads, writes):
            n = len(pairs)
            for i, (l, r) in enumerate(pairs):
                mm(out, l, r, i == 0, i == n - 1, reads, writes, i == n - 1)

        def rstd_from(ss_ap, B_ss, n, nsz=128):
            P.op("act", lambda e: e.activation(out=ss_ap, in_=ss_ap, func=AF.Sqrt, scale=1.0 / n, bias=epsb[0:nsz, 0:1]),
                 reads=[B_ss, B_epsb], writes=[B_ss])
            P.op("dve", lambda e: e.reciprocal(out=ss_ap, in_=ss_ap), reads=[B_ss], writes=[B_ss])

        class Pipe:
            def __init__(self, depth=1):
                self.q = []
                self.depth = depth

            def push(self, fn):
                self.q.append(fn)
                while len(self.q) > self.depth:
                    self.q.pop(0)()

            def flush(self):
                while self.q:
                    self.q.pop(0)()

        def norm_part(R, xs, B_xs, g_ap, B_g, nsz=128):
            ss, B_ss = R["small"].next()
            jk, B_jk = R["junk"].next()
            P.op("act", lambda e: e.activation(out=jk[0:nsz, :], in_=xs, func=AF.Square, accum_out=ss[0:nsz, 0:1]),
                 reads=[B_xs], writes=[B_jk, B_ss])
            rstd_from(ss[0:nsz, 0:1], B_ss, D, nsz)
            hb, B_hb = R["hb"].next()
            P.op("dve", lambda e: e.scalar_tensor_tensor(out=hb[0:nsz, :], in0=xs, scalar=ss[0:nsz, 0:1], in1=g_ap,
                                                          op0=ALU.mult, op1=ALU.mult),
                 reads=[B_xs, B_ss, B_g], writes=[B_hb])
            return hb, B_hb

        def transpose_part(R, hb, B_hb, hT, B_hT, col0, nsz=128):
            for q in range(4):
                tb, B_tb = R["pb"].next()
                for i in range(4):
                    kc = 4 * q + i
                    P.op("pe", lambda e, i=i, kc=kc, tb=tb: e.transpose(tb[:, i, 0:nsz], hb[0:nsz, kc * 128:(kc + 1) * 128], identb[0:nsz, 0:nsz]),
                         reads=[B_hb, B_cbf], writes=[B_tb], signal=(i == 3))
                P.op("act", lambda e, q=q, tb=tb: e.activation(out=hT[:, 4 * q:4 * q + 4, col0:col0 + nsz], in_=tb[:, 0:4, 0:nsz], func=AF.Copy),
                     reads=[B_tb], writes=[B_hT])

        def norm_transpose(R, xs, B_xs, g_ap, B_g, hT, B_hT, col0):
            hb, B_hb = norm_part(R, xs, B_xs, g_ap, B_g)
            transpose_part(R, hb, B_hb, hT, B_hT, col0)

        def load_x_block(R, xap, t0, nsub):
            hs = []
            for s in range(nsub):
                xs, B_xs = R["xst"].next()
                r0 = t0 + s * 128
                P.dma_multi("sp", [(xs[:, c_:c_ + 512], xap[r0:r0 + 128, c_:c_ + 512]) for c_ in range(0, D, 512)], writes=[B_xs])
                hs.append((xs, B_xs))
            return hs

        def subtiles(nt):
            return [(s, min(128, nt - s * 128)) for s in range((nt + 127) // 128)]

        def to_token_major(R, src_bank, B_src, Rt, B_Rt, c, nt):
            mst, B_mst = R["mst"].next()
            P.op("act", lambda e: e.activation(out=mst[:, 0:nt], in_=src_bank[:, 0:nt], func=AF.Copy),
                 reads=[B_src], writes=[B_mst])
            tb, B_tb = R["pf"].next()
            tbv = tb[:].rearrange("p (a b) -> p a b", a=4)
            st = subtiles(nt)
            for (s, nsz) in st:
                P.op("pe", lambda e, s=s, nsz=nsz: e.transpose(tbv[0:nsz, s, :], mst[:, s * 128:s * 128 + nsz], identf[:]),
                     reads=[B_mst, B_identf], writes=[B_tb], signal=(s == st[-1][0]))
            nfull = nt // 128
            if nfull:
                P.op("dve", lambda e: e.tensor_copy(out=Rt[:, 0:nfull, c * 128:(c + 1) * 128], in_=tbv[:, 0:nfull, :]),
                     reads=[B_tb], writes=[B_Rt])
            if nt % 128:
                nsz = nt % 128
                P.op("dve", lambda e: e.tensor_copy(out=Rt[0:nsz, nfull, c * 128:(c + 1) * 128], in_=tbv[0:nsz, nfull, :]),
                     reads=[B_tb], writes=[B_Rt])

        def resid_norm(R, Rt, B_Rt, s, xres, B_xres, g_ap, B_g, nsz=128):
            ss, B_ss = R["small"].next()
            jk, B_jk = R["junk"].next()
            P.op("act", lambda e: e.activation(out=jk[0:nsz, :], in_=Rt[0:nsz, s, :], func=AF.Square, accum_out=ss[0:nsz, 0:1]),
                 reads=[B_Rt], writes=[B_jk, B_ss])
            rstd_from(ss[0:nsz, 0:1], B_ss, D, nsz)
            P.op("dve", lambda e: e.scalar_tensor_tensor(out=Rt[0:nsz, s, :], in0=Rt[0:nsz, s, :], scalar=ss[0:nsz, 0:1], in1=g_ap,
                                                          op0=ALU.mult, op1=ALU.mult),
                 reads=[B_Rt, B_ss, B_g], writes=[B_Rt])
            P.op("dve", lambda e: e.tensor_tensor(out=Rt[0:nsz, s, :], in0=Rt[0:nsz, s, :], in1=xres, op=ALU.add),
                 reads=[B_Rt, B_xres], writes=[B_Rt])

        with contextlib.ExitStack() as ps:
            R = {}
            R["xst"] = Ring(ps, nc, "xst", [128, D], F32, 4)
            R["junk"] = Ring(ps, nc, "junk", [128, D], BF16, 1)
            R["hb"] = Ring(ps, nc, "hb", [128, D], BF16, 4)
            R["small"] = Ring(ps, nc, "small", [128, 1], F32, 8)
            R["hT"] = Ring(ps, nc, "hT", [128, NKC, 512], BF16, 2)
            R["w"] = Ring(ps, nc, "wsl", [128, NKC, 256], BF16, 4)
            R["tab"] = Ring(ps, nc, "tab", [128, 4, 512], F32, 2)
            R["tf"] = Ring(ps, nc, "tf", [128, 512], F32, 6)
            R["tb"] = Ring(ps, nc, "tb", [128, 512], BF16, 4)
            R["ob"] = Ring(ps, nc, "ob", [128, 512], BF16, 4)
            R["vst"] = Ring(ps, nc, "vst", [128, 256], BF16, 4)
            R["pf"] = Ring(ps, nc, "pf", [128, 512], F32, 6, psum=True)
            R["pb"] = Ring(ps, nc, "pb", [128, 8, 128], BF16, 2, psum=True)
            gbc0 = sb(ps, "gbc0", [128, D]); B_g0 = Buf()
            P.dma("sp", gbc0[:], gbc_d[0, :, :], writes=[B_g0])

            for j in J:
                S, NQ = j["S"], j["NQ"]
                JB = j["B"]
                blocks = [(t0, 512) for t0 in range(0, S, 512)]
                xl = load_x_block(R, j["x"], 0, 4)
                hbs = [norm_part(R, xl[s][0][:], xl[s][1], gbc0[:], B_g0) for s in range(4)]
                nxt_hT = R["hT"].next()
                for s in range(4):
                    transpose_part(R, hbs[s][0], hbs[s][1], nxt_hT[0], nxt_hT[1], s * 128)
                for bi, (t0, nt) in enumerate(blocks):
                    own = t0 < NQ
                    hT, B_hT = nxt_hT
                    has_next = bi + 1 < len(blocks)
                    tab, B_tab = R["tab"].next()
                    P.dma("sp", tab[:], j["tab"][:, :, t0:t0 + 512].rearrange("f p n -> p f n"), writes=[B_tab])

                    slabs = []
                    if own:
                        slabs += [("qA", c0) for c0 in (0, 256, 512, 768)]
                    slabs += [("kA", 1024), ("vA", 1280)]
                    if own:
                        slabs += [("qB", c0) for c0 in (1536, 1792, 2048, 2304)]
                    slabs += [("kB", 2560), ("vB", 2816)]
                    pipe = Pipe()
                    vpos = [i_ for i_, (k_, _) in enumerate(slabs) if k_ == "vA"][0]
                    if vpos >= 4:
                        ldsched = {vpos - 4: [0], vpos - 3: [1], vpos - 2: [2], vpos - 1: [3]}
                    else:
                        ldsched = {0: [0, 1], vpos: [2, 3]}
                    if has_next:
                        hbs = [None] * 4
                        xls = [None] * 4
                    for si, (kind, c0) in enumerate(slabs):
                        if has_next and si in ldsched:
                            for s_ in ldsched[si]:
                                (xls[s_],) = load_x_block(R, j["x"], blocks[bi + 1][0] + s_ * 128, 1)
                        if has_next and si == vpos:
                            for s_ in range(4):
                                hbs[s_] = norm_part(R, xls[s_][0][:], xls[s_][1], gbc0[:], B_g0)
                        if has_next and si == len(slabs) - 2:
                            nxt_hT = R["hT"].next()
                            for s in range(4):
                                transpose_part(R, hbs[s][0], hbs[s][1], nxt_hT[0], nxt_hT[1], s * 128)
                        wt, B_wt = R["w"].next()
                        P.dma("pool", wt[:], w_in_v[:, :, c0:c0 + 256], writes=[B_wt])
                        if kind[0] == "v":
                            vd, B_vd = (j["VA"], JB["VA"]) if kind == "vA" else (j["VB"], JB["VB"])
                            for s in range(4):
                                bk, B_bk = R["pf"].next()
                                mm_group(bk[:, 0:256], [(hT[:, kc, s * 128:(s + 1) * 128], wt[:, kc, :]) for kc in range(NKC)],
                                         [B_hT, B_wt], [B_bk])

                                def vpost(bk=bk, B_bk=B_bk, s=s, vd=vd, B_vd=B_vd):
                                    vs, B_vs = R["vst"].next()
                                    P.op("act", lambda e: e.activation(out=vs[:], in_=bk[:, 0:256], func=AF.Copy),
                                         reads=[B_bk], writes=[B_vs])
                                    r0 = t0 + s * 128
                                    P.dma("sp", vd[r0:r0 + 128, :], vs[:], reads=[B_vs], writes=[B_vd])
                                if DBG not in ("T3", "T5"):
                                    pipe.push(vpost)
                            continue
                        for half in range(2):
                            bk, B_bk = R["pf"].next()
                            mm_group(bk[:], [(wt[:, kc, half * 128:(half + 1) * 128], hT[:, kc, :]) for kc in range(NKC)],
                                     [B_hT, B_wt], [B_bk])
                            if kind == "qA":
                                idx = c0 // 128 + half
                                dst, B_dst = j["QAT"][idx, :, t0:t0 + 512], JB["QAT"]
                            elif kind == "qB":
                                idx = (c0 - 1536) // 128 + half
                                dst, B_dst = j["QBT"][idx, :, t0:t0 + 512], JB["QBT"]
                            elif kind == "kA":
                                dst, B_dst = j["KAT"][half, :, t0:t0 + 512], JB["KAT"]
                            else:
                                dst, B_dst = j["KBT"][half, :, t0:t0 + 512], JB["KBT"]
                            isA = kind[1] == "A"

                            def post(bk=bk, B_bk=B_bk, isA=isA, kind=kind, dst=dst, B_dst=B_dst, tab=tab, B_tab=B_tab):
                                qg, B_qg = R["tb"].next()
                                t1, B_t1 = R["tf"].next()
                                t2, B_t2 = R["tf"].next()
                                ob, B_ob = R["ob"].next()
                                b3, B_b3 = R["pf"].next()
                                if isA:
                                    sq, B_sq = R["tf"].next()
                                    col = 0 if kind[0] == "q" else 1
                                    P.op("act", lambda e: e.activation(out=sq[:], in_=bk[:], func=AF.Square),
                                         reads=[B_bk], writes=[B_sq])
                                    P.op("act", lambda e: e.activation(out=qg[:], in_=bk[:], func=AF.Copy, scale=qkn[:, col:col + 1]),
                                         reads=[B_bk, B_qkn], writes=[B_qg])
                                    b2, B_b2 = R["pf"].next()
                                    mm(b2[:], onesf[:], sq[:], True, True, [B_onesf, B_sq], [B_b2], True)
                                    mm(b3[:], rotA, qg[:], True, True, [B_cbf, B_qg], [B_b3], True)
                                    rs, B_rs = sq, B_sq
                                    P.op("act", lambda e: e.activation(out=rs[:], in_=b2[:], func=AF.Ln, scale=1.0 / HD, bias=epsb[:, 0:1]),
                                         reads=[B_b2, B_epsb], writes=[B_rs])
                                    P.op("act", lambda e: e.activation(out=rs[:], in_=rs[:], func=AF.Exp, scale=-0.5), reads=[B_rs], writes=[B_rs])
                                    P.op("dve", lambda e: e.tensor_tensor(out=t1[:], in0=qg[:], in1=tab[:, 0, :], op=ALU.mult),
                                         reads=[B_qg, B_tab], writes=[B_t1])
                                    P.op("dve", lambda e: e.tensor_tensor(out=t2[:], in0=b3[:], in1=tab[:, 1, :], op=ALU.mult),
                                         reads=[B_b3, B_tab], writes=[B_t2])
                                    P.op("dve", lambda e: e.tensor_tensor(out=t1[:], in0=t1[:], in1=t2[:], op=ALU.add),
                                         reads=[B_t1, B_t2], writes=[B_t1])
                                    P.op("dve", lambda e: e.tensor_tensor(out=ob[:], in0=t1[:], in1=rs[:], op=ALU.mult),
                                         reads=[B_t1, B_rs], writes=[B_ob])
                                else:
                                    P.op("act", lambda e: e.activation(out=qg[:], in_=bk[:], func=AF.Copy),
                                         reads=[B_bk], writes=[B_qg])
                                    mm(b3[:], rotB, qg[:], True, True, [B_cbf, B_qg], [B_b3], True)
                                    P.op("dve", lambda e: e.tensor_tensor(out=t1[:], in0=bk[:], in1=tab[:, 2, :], op=ALU.mult),
                                         reads=[B_bk, B_tab], writes=[B_t1])
                                    P.op("dve", lambda e: e.tensor_tensor(out=t2[:], in0=b3[:], in1=tab[:, 3, :], op=ALU.mult),
                                         reads=[B_b3, B_tab], writes=[B_t2])
                                    P.op("dve", lambda e: e.tensor_tensor(out=ob[:], in0=t1[:], in1=t2[:], op=ALU.add),
                                         reads=[B_t1, B_t2], writes=[B_ob])
                                P.dma("sp", dst, ob[:], reads=[B_ob], writes=[B_dst])
                            if DBG not in ("T3", "T4"):
                                pipe.push(post)
                    pipe.flush()
            P.barrier()
            P.emit()

        if STOP_AFTER[0] < 2:
            return nc
        with contextlib.ExitStack() as ps:
            R = {}
            NKB = SMAX // 128
            R["K"] = Ring(ps, nc, "Kt", [128, SMAX], BF16, 2)
            R["V"] = Ring(ps, nc, "Vt", [128, NKB, 128], BF16, 2)
            R["Q"] = Ring(ps, nc, "Qt", [128, 4, 512], BF16, 2)
            R["QB"] = Ring(ps, nc, "QBt", [128, 8, 512], BF16, 2)
            R["obst"] = Ring(ps, nc, "obst", [128, 8, 512], BF16, 1)
            R["pt"] = Ring(ps, nc, "pt", [128, 2, 512], BF16, 6)
            R["ptB"] = Ring(ps, nc, "ptB", [128, 512], BF16, 6)
            R["accD"] = Ring(ps, nc, "accD", [128, 2, DSPLIT], F32, 2)
            R["accP"] = Ring(ps, nc, "accP", [128, 2, 512 - DSPLIT], F32, 2)
            R["tf"] = Ring(ps, nc, "tf", [128, 512], F32, 3)
            R["ob"] = Ring(ps, nc, "ob", [128, 512], BF16, 2)
            R["pS2"] = Ring(ps, nc, "pS2", [128, 2, 512], F32, 2, psum=True)
            R["pO"] = Ring(ps, nc, "pO", [128, 512], F32, 2, psum=True)
            R["pM"] = Ring(ps, nc, "pM", [128, 512], F32, 2, psum=True)
            KB0 = sb(ps, "KB0", [128, SMAX], BF16); KB1 = sb(ps, "KB1", [128, SMAX], BF16)
            VBt = sb(ps, "VBt", [128, NKB, 256], BF16)
            B_KB = [Buf(), Buf()]; B_VBt = Buf()
            sinkrow = sb(ps, "sinkrow", [128, 2, 512]); B_sinkrow = Buf()
            zer = sb(ps, "zer", [128, 128]); B_zer = Buf()
            P.op("dve", lambda e: e.memset(zer[:], 0.0), writes=[B_zer])
            for h in range(8):
                P.op("act", lambda e, h=h: e.activation(out=sinkrow[:, h // 4, (h % 4) * 128:(h % 4 + 1) * 128], in_=zer[:],
                                                        func=AF.Identity, bias=expsink[:, h:h + 1]),
                     reads=[B_zer, B_expsink], writes=[B_sinkrow])
            KBs = [KB0, KB1]

            def make_bunit(j, t0, nt, sj, g, first, lastu, blk, nkb):
                qbi = t0 // 128 + sj
                kbs = [k for k in (qbi - 1, qbi, qbi + 1) if 0 <= k < nkb]
                st = {}

                def s1():
                    if first:
                        blk["QB"] = R["QB"].next()
                        blk["obst"] = R["obst"].next()
                        QB, B_QB = blk["QB"]
                        P.dma("sp", QB[:, :, 0:nt], j["QBT"][:, :, t0:t0 + nt].rearrange("h p n -> p h n"), writes=[B_QB])
                    QB, B_QB = blk["QB"]
                    pts = []
                    for k in kbs:
                        sbk2, B_sbk = R["pS2"].next()
                        sbk = sbk2[:, 0, :]
                        sbv = sbk.rearrange("p (a b) -> p a b", a=4)
                        mm(sbv, KBs[g][:, k * 128:(k + 1) * 128], QB[:, 4 * g:4 * g + 4, sj * 128:(sj + 1) * 128], True, True,
                           [B_KB[g], B_QB], [B_sbk], True)
                        pt, B_pt = R["ptB"].next()
                        P.op("act", lambda e, pt=pt, sbk=sbk: e.activation(out=pt[:], in_=sbk, func=AF.Exp, scale=SCALE),
                             reads=[B_sbk], writes=[B_pt])
                        if k != qbi:
                            msk = maskP if k < qbi else maskN
                            P.op("dve", lambda e, pt=pt, msk=msk: e.tensor_tensor(out=pt[:], in0=pt[:], in1=msk, op=ALU.mult),
                                 reads=[B_pt, B_cbf], writes=[B_pt])
                        pts.append((k, pt, B_pt))
                    st["pts"] = pts

                def s2():
                    pts = st["pts"]
                    obst, B_obst = blk["obst"]
                    O, B_O = R["pO"].next()
                    M, B_M = R["pM"].next()
                    n = len(pts)
                    for ii, (k, pt, B_pt) in enumerate(pts):
                        last = ii == n - 1
                        mm(O[:], VBt[:, k, g * 128:(g + 1) * 128], pt[:], ii == 0, last, [B_VBt, B_pt], [B_O], last)
                        mm(M[:], onesb, pt[:], ii == 0, last, [B_cbf, B_pt], [B_M], last)
                    rc, B_rc = R["tf"].next()
                    P.op("dve", lambda e: e.tensor_tensor(out=rc[:], in0=M[:], in1=sinkrow[:, g, :], op=ALU.add),
                         reads=[B_M, B_sinkrow], writes=[B_rc])
                    P.op("act", lambda e: e.activation(out=rc[:], in_=rc[:], func=AF.Ln), reads=[B_rc], writes=[B_rc])
                    P.op("act", lambda e: e.activation(out=rc[:], in_=rc[:], func=AF.Exp, scale=-1.0), reads=[B_rc], writes=[B_rc])
                    P.op("dve", lambda e: e.tensor_tensor(
                        out=obst[:, 4 * g:4 * g + 4, sj * 128:(sj + 1) * 128],
                        in0=O[:].rearrange("p (a b) -> p a b", a=4),
                        in1=rc[:].rearrange("p (a b) -> p a b", a=4), op=ALU.mult),
                         reads=[B_O, B_rc], writes=[B_obst])
                    if lastu:
                        P.dma("sp", j["OBT"][:, :, t0:t0 + nt].rearrange("h p n -> p h n"), obst[:, :, 0:nt], reads=[B_obst])
                return s1, s2

            for j in J:
                S, NQ = j["S"], j["NQ"]
                nkb = S // 128
                qblocks = [(t0, min(512, NQ - t0)) for t0 in range(0, NQ, 512)]
                for g in range(2):
                    P.dma("sp", KBs[g][:, 0:S], j["KBT"][g, :, :], writes=[B_KB[g]])
                P.dma("sp", VBt[:, 0:nkb, :], j["VB"].rearrange("(kb p) c -> p kb c", p=128), writes=[B_VBt])
                bunits = []
                for (t0, nt) in qblocks:
                    nsub = nt // 128
                    blk = {}
                    for sj in range(nsub):
                        for g in range(2):
                            bunits.append(make_bunit(j, t0, nt, sj, g, sj == 0 and g == 0, sj == nsub - 1 and g == 1, blk, nkb))
                ui = 0
                for g in range(2):
                    Kt, B_Kt = R["K"].next()
                    Vt, B_Vt = R["V"].next()
                    P.dma("sp", Kt[:, 0:S], j["KAT"][g, :, :], writes=[B_Kt])
                    P.dma("sp", Vt[:, 0:nkb, :], j["VA"][:, g * 128:(g + 1) * 128].rearrange("(kb p) d -> p kb d", p=128), writes=[B_Vt])
                    for (t0, nt) in qblocks:
                        Qt, B_Qt = R["Q"].next()
                        P.dma("sp", Qt[:, :, 0:nt], j["QAT"][4 * g:4 * g + 4, :, t0:t0 + nt].rearrange("h p n -> p h n"), writes=[B_Qt])
                        for hh in range(4):
                            bu = bunits[ui] if ui < len(bunits) else None
                            ui += 1
                            issue_late_casts(2)
                            if bu is not None:
                                bu[0]()
                            O, B_O = R["pO"].next()
                            accD, B_accD = R["accD"].next()
                            accP, B_accP = R["accP"].next()
                            dsp = min(DSPLIT, nt)
                            pipe = Pipe(3)
                            assert nkb % 2 == 0
                            for kp in range(nkb // 2):
                                sb2, B_sb2 = R["pS2"].next()
                                for u in range(2):
                                    kb = 2 * kp + u
                                    mm(sb2[:, u, 0:nt], Kt[:, kb * 128:(kb + 1) * 128], Qt[:, hh, 0:nt], True, True,
                                       [B_Kt, B_Qt], [B_sb2], u == 1)
                                pt, B_pt = R["pt"].next()
                                P.op("act", lambda e: e.activation(out=pt[:, :, 0:nt], in_=sb2[:, :, 0:nt], func=AF.Exp, scale=SCALE),
                                     reads=[B_sb2], writes=[B_pt])
                                if kp == 0:
                                    P.op("dve", lambda e: e.tensor_copy(out=accD[:, :, 0:dsp], in_=pt[:, :, 0:dsp]), reads=[B_pt], writes=[B_accD])
                                    if nt > dsp:
                                        P.op("pool", lambda e: e.tensor_copy(out=accP[:, :, 0:nt - dsp], in_=pt[:, :, dsp:nt]), reads=[B_pt], writes=[B_accP])
                                else:
                                    P.op("dve", lambda e: e.tensor_tensor(out=accD[:, :, 0:dsp], in0=accD[:, :, 0:dsp], in1=pt[:, :, 0:dsp], op=ALU.add),
                                         reads=[B_pt, B_accD], writes=[B_accD])
                                    if nt > dsp:
                                        P.op("pool", lambda e: e.tensor_tensor(out=accP[:, :, 0:nt - dsp], in0=accP[:, :, 0:nt - dsp], in1=pt[:, :, dsp:nt], op=ALU.add),
                                             reads=[B_pt, B_accP], writes=[B_accP])

                                def pv(kp=kp, pt=pt, B_pt=B_pt, O=O, B_O=B_O):
                                    for u in range(2):
                                        kb = 2 * kp + u
                                        last = kb == nkb - 1
                                        mm(O[:, 0:nt], Vt[:, kb, :], pt[:, u, 0:nt], kb == 0, last, [B_Vt, B_pt], [B_O], last)
                                pipe.push(pv)
                            pipe.flush()
                            M, B_M = R["pM"].next()
                            mm(M[:, 0:dsp], onesf[:], accD[:, 0, 0:dsp], True, False, [B_onesf, B_accD], [B_M], False)
                            mm(M[:, 0:dsp], onesf[:], accD[:, 1, 0:dsp], False, True, [B_onesf, B_accD], [B_M], True)
                            if nt > dsp:
                                mm(M[:, dsp:nt], onesf[:], accP[:, 0, 0:nt - dsp], True, False, [B_onesf, B_accP], [B_M], False)
                                mm(M[:, dsp:nt], onesf[:], accP[:, 1, 0:nt - dsp], False, True, [B_onesf, B_accP], [B_M], True)
                            rc, B_rc = R["tf"].next()
                            ob, B_ob = R["ob"].next()
                            P.op("act", lambda e, rc=rc, M=M: e.activation(out=rc[:, 0:nt], in_=M[:, 0:nt], func=AF.Ln), reads=[B_M], writes=[B_rc])
                            P.op("act", lambda e, rc=rc: e.activation(out=rc[:, 0:nt], in_=rc[:, 0:nt], func=AF.Exp, scale=-1.0), reads=[B_rc], writes=[B_rc])
                            P.op("dve", lambda e, rc=rc, O=O, ob=ob: e.tensor_tensor(out=ob[:, 0:nt], in0=O[:, 0:nt], in1=rc[:, 0:nt], op=ALU.mult),
                                 reads=[B_O, B_rc], writes=[B_ob])
                            P.dma("sp", j["OAT"][4 * g + hh, :, t0:t0 + nt], ob[:, 0:nt], reads=[B_ob])
                            if bu is not None:
                                bu[1]()
                while ui < len(bunits):
                    bunits[ui][0]()
                    bunits[ui][1]()
                    ui += 1
            issue_late_casts(len(late_casts))
            P.barrier()
            P.emit()

        if STOP_AFTER[0] < 3:
            return nc
        with contextlib.ExitStack() as ps:
            R = {}
            R["xst"] = Ring(ps, nc, "xst", [128, D], F32, 2)
            R["junk"] = Ring(ps, nc, "junk", [128, D], BF16, 1)
            R["hb"] = Ring(ps, nc, "hb", [128, D], BF16, 3)
            R["small"] = Ring(ps, nc, "small", [128, 1], F32, 8)
            R["w"] = Ring(ps, nc, "wsl", [128, NKC, 256], BF16, 3)
            R["wb"] = Ring(ps, nc, "wbr", [128, 8, 256], BF16, 3)
            R["tf"] = Ring(ps, nc, "tf", [128, 512], F32, 4)
            R["mst"] = Ring(ps, nc, "mst", [128, 512], F32, 2)
            R["pf"] = Ring(ps, nc, "pf", [128, 512], F32, 6, psum=True)
            R["pb"] = Ring(ps, nc, "pb", [128, 8, 128], BF16, 2, psum=True)
            hT = sb(ps, "hT3", [128, NKC, 512], BF16); B_hT = Buf()
            h2st = sb(ps, "h2st", [128, NKC, 512], BF16); B_h2st = Buf()
            oA = sb(ps, "oA", [128, 8, 512], BF16); B_oA = Buf()
            oB = sb(ps, "oB", [128, 8, 512], BF16); B_oB = Buf()
            mg = sb(ps, "mg", [128, NKC, 512], BF16); B_mg = Buf()
            Rt = sb(ps, "Rt", [128, 4, D]); B_Rt = Buf()
            g0 = sb(ps, "g0", [128, D]); g1 = sb(ps, "g1", [128, D]); g2 = sb(ps, "g2", [128, D])
            B_gs = [Buf(), Buf(), Buf()]
            for i, gt in enumerate((g0, g1, g2)):
                P.dma("sp", gt[:], gbc_d[i, :, :], writes=[B_gs[i]])

            blocks3 = [(j, t0, min(512, j["NQ"] - t0)) for j in J for t0 in range(0, j["NQ"], 512)]

            def head_a(blk, s_):
                j, t0, nt = blk
                (xs_, B_xs_), = load_x_block(R, j["x"], t0 + s_ * 128, 1)
                return norm_part(R, xs_[:], B_xs_, g0[:], B_gs[0])

            def head_b(blk, s_, hb_):
                transpose_part(R, hb_[0], hb_[1], hT, B_hT, s_ * 128)

            def head_o(blk):
                j, t0, nt = blk
                P.dma("sp", oA[:, :, 0:nt], j["OAT"][:, :, t0:t0 + nt].rearrange("h p n -> p h n"), writes=[B_oA])
                P.dma("sp", oB[:, :, 0:nt], j["OBT"][:, :, t0:t0 + nt].rearrange("h p n -> p h n"), writes=[B_oB])

            def epi_a(blk, s_):
                j, t0, nt = blk
                (xs_, B_xs_), = load_x_block(R, j["x"], t0 + s_ * 128, 1)
                resid_norm(R, Rt, B_Rt, s_, xs_[:], B_xs_, g1[:], B_gs[1])
                r0 = t0 + s_ * 128
                P.dma_multi("sp", [(j["X1"][r0:r0 + 128, c_:c_ + 512], Rt[:, s_, c_:c_ + 512]) for c_ in range(0, D, 512)], reads=[B_Rt])
                return norm_part(R, Rt[:, s_, :], B_Rt, g2[:], B_gs[2])

            def epi_b(blk, s_, hb_):
                j, t0, nt = blk
                transpose_part(R, hb_[0], hb_[1], h2st, B_h2st, s_ * 128)
                if s_ == nt // 128 - 1:
                    P.dma("sp", j["H2T"][:, :, t0:t0 + nt].rearrange("k p n -> p k n"), h2st[:, :, 0:nt], reads=[B_h2st])

            for s_ in range(blocks3[0][2] // 128):
                head_b(blocks3[0], s_, head_a(blocks3[0], s_))
            head_o(blocks3[0])

            for bi, blk in enumerate(blocks3):
                j, t0, nt = blk
                nsub = nt // 128
                prev = blocks3[bi - 1] if bi > 0 else None
                nxt = blocks3[bi + 1] if bi + 1 < len(blocks3) else None
                ehb = {}
                pipe = Pipe()
                for cp in range(8):
                    if prev is not None:
                        pn = prev[2] // 128
                        if 0 <= cp - 2 < pn:
                            epi_b(prev, cp - 2, ehb[cp - 2])
                        if cp < pn:
                            ehb[cp] = epi_a(prev, cp)
                    wga, B_wga = R["w"].next()
                    P.dma("pool", wga[:], w_in_v[:, :, 3072 + cp * 256:3072 + (cp + 1) * 256], writes=[B_wga])
                    wgb, B_wgb = R["w"].next()
                    P.dma("pool", wgb[:], w_in_v[:, :, 5120 + cp * 256:5120 + (cp + 1) * 256], writes=[B_wgb])
                    wba, B_wba = R["wb"].next()
                    P.dma("pool", wba[:], w_ba_v[:, :, cp * 256:(cp + 1) * 256], writes=[B_wba])
                    wbb, B_wbb = R["wb"].next()
                    P.dma("pool", wbb[:], w_bb_v[:, :, cp * 256:(cp + 1) * 256], writes=[B_wbb])
                    for half in range(2):
                        c = 2 * cp + half
                        cs = slice(half * 128, (half + 1) * 128)
                        bga, B_bga = R["pf"].next()
                        mm_group(bga[:, 0:nt], [(wga[:, kc, cs], hT[:, kc, 0:nt]) for kc in range(NKC)], [B_wga, B_hT], [B_bga])
                        bgb, B_bgb = R["pf"].next()
                        mm_group(bgb[:, 0:nt], [(wgb[:, kc, cs], hT[:, kc, 0:nt]) for kc in range(NKC)], [B_wgb, B_hT], [B_bgb])
                        sa, B_sa = R["tf"].next()
                        sbb, B_sbb = R["tf"].next()

                        def post_gate(bga=bga, B_bga=B_bga, bgb=bgb, B_bgb=B_bgb, sa=sa, B_sa=B_sa, sbb=sbb, B_sbb=B_sbb, nt=nt):
                            P.op("act", lambda e: e.activation(out=sa[:, 0:nt], in_=bga[:, 0:nt], func=AF.Sigmoid), reads=[B_bga], writes=[B_sa])
                            P.op("act", lambda e: e.activation(out=sbb[:, 0:nt], in_=bgb[:, 0:nt], func=AF.Sigmoid), reads=[B_bgb], writes=[B_sbb])
                        pipe.push(post_gate)
                        bba, B_bba = R["pf"].next()
                        mm_group(bba[:, 0:nt], [(wba[:, kc, cs], oA[:, kc, 0:nt]) for kc in range(8)], [B_wba, B_oA], [B_bba])
                        bbb, B_bbb = R["pf"].next()
                        mm_group(bbb[:, 0:nt], [(wbb[:, kc, cs], oB[:, kc, 0:nt]) for kc in range(8)], [B_wbb, B_oB], [B_bbb])

                        def post_br(c=c, bba=bba, B_bba=B_bba, bbb=bbb, B_bbb=B_bbb, sa=sa, B_sa=B_sa, sbb=sbb, B_sbb=B_sbb, nt=nt):
                            P.op("dve", lambda e: e.tensor_tensor(out=sa[:, 0:nt], in0=bba[:, 0:nt], in1=sa[:, 0:nt], op=ALU.mult),
                                 reads=[B_bba, B_sa], writes=[B_sa])
                            P.op("dve", lambda e: e.tensor_tensor(out=sbb[:, 0:nt], in0=bbb[:, 0:nt], in1=sbb[:, 0:nt], op=ALU.mult),
                                 reads=[B_bbb, B_sbb], writes=[B_sbb])
                            P.op("dve", lambda e: e.tensor_tensor(out=mg[:, c, 0:nt], in0=sa[:, 0:nt], in1=sbb[:, 0:nt], op=ALU.add),
                                 reads=[B_sa, B_sbb], writes=[B_mg])
                        pipe.push(post_br)
                pipe.flush()
                hhb = {}
                for cp in range(8):
                    if nxt is not None:
                        nn = nxt[2] // 128
                        if cp == 0:
                            head_o(nxt)
                        if 0 <= cp - 2 < nn:
                            head_b(nxt, cp - 2, hhb[cp - 2])
                        if cp < nn:
                            hhb[cp] = head_a(nxt, cp)
                    wo, B_wo = R["w"].next()
                    P.dma("pool", wo[:], w_out_v[:, :, cp * 256:(cp + 1) * 256], writes=[B_wo])
                    for half in range(2):
                        c = 2 * cp + half
                        cs = slice(half * 128, (half + 1) * 128)
                        bk, B_bk = R["pf"].next()
                        mm_group(bk[:, 0:nt], [(wo[:, kc, cs], mg[:, kc, 0:nt]) for kc in range(NKC)], [B_wo, B_mg], [B_bk])
                        pipe.push(lambda c=c, bk=bk, B_bk=B_bk, nt=nt: to_token_major(R, bk, B_bk, Rt, B_Rt, c, nt))
                pipe.flush()
            last = blocks3[-1]
            for s_ in range(last[2] // 128):
                epi_b(last, s_, epi_a(last, s_))
            P.barrier()
            P.emit()

        if STOP_AFTER[0] < 4:
            return nc
        with contextlib.ExitStack() as ps:
            R = {}
            R["junk"] = Ring(ps, nc, "junk", [128, D], BF16, 1)
            R["small"] = Ring(ps, nc, "small", [128, 1], F32, 8)
            R["w"] = Ring(ps, nc, "wsl", [128, NKC, 256], BF16, 3)
            R["wd"] = Ring(ps, nc, "wdn", [128, NFC, 128], BF16, 2)
            R["ub"] = Ring(ps, nc, "ub", [128, 514], F32, 4)
            R["tf"] = Ring(ps, nc, "tf", [128, 512], F32, 6)
            R["mst"] = Ring(ps, nc, "mst", [128, 512], F32, 2)
            R["x1"] = Ring(ps, nc, "x1s", [128, D], F32, 2)
            R["pf"] = Ring(ps, nc, "pf", [128, 512], F32, 8, psum=True)
            h2T = sb(ps, "h2T", [128, NKC, 514], BF16); B_h2T = Buf()
            gT = sb(ps, "gT", [128, NFC, 512], BF16); B_gT = Buf()
            Rt = sb(ps, "Rt4", [128, 4, D]); B_Rt = Buf()
            g3 = sb(ps, "g3", [128, D]); B_g3 = Buf()
            cw = sb(ps, "cw", [128, 3, 88]); B_cw = Buf()
            P.dma("sp", g3[:], gbc_d[3, :, :], writes=[B_g3])

            blocks4 = []
            for j in J:
                nblk_ = -(-j["NOUT"] // 510)
                base_ = -(-j["NOUT"] // nblk_)
                t = 0
                while t < j["NOUT"]:
                    n = min(base_, j["NOUT"] - t)
                    blocks4.append((j, t, n))
                    t += n

            def load_h2T(blk):
                j, t0, nt = blk
                NQ, S = j["NQ"], j["S"]
                W_ = nt + 2
                lo, hi = t0 - 1, t0 + nt + 1
                c_lo, c_hi = 0, W_
                if lo < 0:
                    P.op("dve", lambda e: e.memset(h2T[:, :, 0:1], 0.0), writes=[B_h2T])
                    lo, c_lo = 0, 1
                if hi > NQ:
                    assert hi - 1 == S, "right halo missing"
                    P.op("dve", lambda e: e.memset(h2T[:, :, W_ - 1:W_], 0.0), writes=[B_h2T])
                    hi, c_hi = hi - 1, W_ - 1
                P.dma("sp", h2T[:, :, c_lo:c_hi], j["H2T"][:, :, lo:hi].rearrange("k p n -> p k n"), writes=[B_h2T])

            def epi4(blk, s_, nsz):
                j, t0, nt = blk
                x1s, B_x1s = R["x1"].next()
                r0 = t0 + s_ * 128
                P.dma_multi("sp", [(x1s[0:nsz, c_:c_ + 512], j["X1"][r0:r0 + nsz, c_:c_ + 512]) for c_ in range(0, D, 512)], writes=[B_x1s])
                resid_norm(R, Rt, B_Rt, s_, x1s[0:nsz, :], B_x1s, g3[0:nsz, :], B_g3, nsz)
                P.dma_multi("sp", [(j["y"][r0:r0 + nsz, c_:c_ + 512], Rt[0:nsz, s_, c_:c_ + 512]) for c_ in range(0, D, 512)], reads=[B_Rt])

            load_h2T(blocks4[0])
            cur_job = None
            for bi, blk in enumerate(blocks4):
                j, t0, nt = blk
                if j is not cur_job:
                    P.dma("sp", cw[:], j["convw"][:, :, :], writes=[B_cw])
                    cur_job = j
                prev = blocks4[bi - 1] if bi > 0 else None
                nxt = blocks4[bi + 1] if bi + 1 < len(blocks4) else None
                W_ = nt + 2
                esched = {}
                if prev is not None:
                    for k_, (s_, nsz) in enumerate(subtiles(prev[2])):
                        esched[1 + 3 * k_] = (s_, nsz)
                pipe = Pipe()
                for i in range(NFC):
                    if i in esched:
                        epi4(prev, *esched[i])
                    wu, B_wu = R["w"].next()
                    P.dma("pool", wu[:, :, 0:128], w_up_v[:, :, i * 128:(i + 1) * 128], writes=[B_wu])
                    P.dma("pool", wu[:, :, 128:256], w_up_v[:, :, DFF + i * 128:DFF + (i + 1) * 128], writes=[B_wu])
                    bks = []
                    for half in range(2):
                        cs = slice(half * 128, (half + 1) * 128)
                        Wm = min(W_, 512)
                        bm, B_bm = R["pf"].next()
                        mm_group(bm[:, 0:Wm], [(wu[:, kc, cs], h2T[:, kc, 0:Wm]) for kc in range(NKC)], [B_wu, B_h2T], [B_bm])
                        bt, B_bt = None, None
                        if W_ > 512:
                            bt, B_bt = R["pf"].next()
                            mm_group(bt[:, 0:W_ - 512], [(wu[:, kc, cs], h2T[:, kc, 512:W_]) for kc in range(NKC)], [B_wu, B_h2T], [B_bt])
                        bks.append((bm, B_bm, bt, B_bt))

                    def post(i=i, bks=bks, nt=nt, W_=W_):
                        cv = []
                        for half in range(2):
                            bm, B_bm, bt, B_bt = bks[half]
                            ci = i + half * NFC
                            u, B_u = R["ub"].next()
                            Wm = min(W_, 512)
                            P.op("act", lambda e: e.activation(out=u[:, 0:Wm], in_=bm[:, 0:Wm], func=AF.Copy), reads=[B_bm], writes=[B_u])
                            if bt is not None:
                                P.op("act", lambda e: e.activation(out=u[:, 512:W_], in_=bt[:, 0:W_ - 512], func=AF.Copy), reads=[B_bt], writes=[B_u])
                            a_, B_a = R["tf"].next()
                            P.op("dve", lambda e: e.tensor_scalar(out=a_[:, 0:nt], in0=u[:, 0:nt], scalar1=cw[:, 0, ci:ci + 1], scalar2=convb[:, ci:ci + 1],
                                                                  op0=ALU.mult, op1=ALU.add),
                                 reads=[B_u, B_cw, B_convb], writes=[B_a])
                            P.op("dve", lambda e: e.scalar_tensor_tensor(out=a_[:, 0:nt], in0=u[:, 1:nt + 1], scalar=cw[:, 1, ci:ci + 1], in1=a_[:, 0:nt],
                                                                         op0=ALU.mult, op1=ALU.add),
                                 reads=[B_u, B_cw, B_a], writes=[B_a])
                            P.op("dve", lambda e: e.scalar_tensor_tensor(out=a_[:, 0:nt], in0=u[:, 2:nt + 2], scalar=cw[:, 2, ci:ci + 1], in1=a_[:, 0:nt],
                                                                         op0=ALU.mult, op1=ALU.add),
                                 reads=[B_u, B_cw, B_a], writes=[B_a])
                            cv.append((a_, B_a))
                        (a_, B_a), (b_, B_b) = cv
                        P.op("act", lambda e: e.activation(out=a_[:, 0:nt], in_=a_[:, 0:nt], func=AF.Gelu_apprx_tanh), reads=[B_a], writes=[B_a])
                        P.op("dve", lambda e: e.tensor_tensor(out=gT[:, i, 0:nt], in0=a_[:, 0:nt], in1=b_[:, 0:nt], op=ALU.mult),
                             reads=[B_a, B_b], writes=[B_gT])
                    pipe.push(post)
                pipe.flush()
                for c in range(NKC):
                    if c == 0 and nxt is not None:
                        load_h2T(nxt)
                    wd, B_wd = R["wd"].next()
                    P.dma("pool", wd[:], w_down_v[:, :, c * 128:(c + 1) * 128], writes=[B_wd])
                    bk, B_bk = R["pf"].next()
                    mm_group(bk[:, 0:nt], [(wd[:, kc, :], gT[:, kc, 0:nt]) for kc in range(NFC)], [B_wd, B_gT], [B_bk])
                    pipe.push(lambda c=c, bk=bk, B_bk=B_bk, nt=nt: to_token_major(R, bk, B_bk, Rt, B_Rt, c, nt))
                pipe.flush()
            last = blocks4[-1]
            for (s_, nsz) in subtiles(last[2]):
                epi4(last, s_, nsz)
            P.barrier()
            P.emit()
        nc._prog_stats = (P.n_ops, P.n_wait, dict(P.cnt), max(P.dma_cnt))
    return nc


def _rope_tables(S, reverse):
    t = np.arange(S)
    row = (t // GRID_W).astype(np.float32)
    col = (t % GRID_W).astype(np.float32)
    tf = t.astype(np.float32)
    inv64 = (np.float32(THETA) ** (-(np.arange(0, 64, 2, dtype=np.float32)) / np.float32(64))).astype(np.float32)
    inv128 = (np.float32(THETA) ** (-(np.arange(0, 128, 2, dtype=np.float32)) / np.float32(128))).astype(np.float32)
    angA = np.zeros((128, S), np.float32)
    for d in range(128):
        pos = row if d < 64 else col
        angA[d] = pos * inv64[(d % 64) % 32]
    angB = np.zeros((128, S), np.float32)
    for d in range(128):
        angB[d] = tf * inv128[d % 64]
    tab = np.stack([np.cos(angA.astype(np.float64)), np.sin(angA.astype(np.float64)),
                    np.cos(angB.astype(np.float64)), np.sin(angB.astype(np.float64))]).astype(np.float32)
    if reverse:
        tab = tab[:, :, ::-1]
    return np.ascontiguousarray(tab)


def _consts():
    bf = ml_dtypes.bfloat16
    identb = np.eye(128, dtype=np.float32)
    onesb = np.ones((128, 128), np.float32)
    rotA = np.zeros((128, 128), np.float32)
    for base in (0, 64):
        for m in range(base, base + 32):
            rotA[m + 32, m] = -1.0
        for m in range(base + 32, base + 64):
            rotA[m - 32, m] = 1.0
    rotB = np.zeros((128, 128), np.float32)
    for m in range(64):
        rotB[m + 64, m] = -1.0
    for m in range(64, 128):
        rotB[m - 64, m] = 1.0
    kk = np.arange(128)[:, None]
    qq = np.arange(128)[None, :]
    mP = np.tile((qq <= kk).astype(np.float32), (1, 4))
    mN = np.tile((kk <= qq).astype(np.float32), (1, 4))
    cbf = np.concatenate([identb, onesb, rotA, rotB, mP, mN], axis=1).astype(bf)
    return np.eye(128, dtype=np.float32), np.ascontiguousarray(cbf)


def _convw_layout(cw, reverse):
    if reverse:
        cw = cw[::-1]
    return np.ascontiguousarray(cw.reshape(3, 88, 128).transpose(2, 0, 1))


_CACHE = {}


def _get_program(jobs_key):
    if jobs_key not in _CACHE:
        jobs = [dict(name=n, S=S, NQ=NQ, NOUT=NOUT) for (n, S, NQ, NOUT) in jobs_key]
        _CACHE[jobs_key] = build_program(jobs)
    return _CACHE[jobs_key]


def run_cores(core_inputs, jobs_key):
    raise NotImplementedError


def kernel(x_prompt, x_sample, norm_pre_mix, w_in, q_norm_a, k_norm_a, sink_b, w_branch_a,
           w_branch_b, w_out, norm_post_mix, norm_pre_ffn, w_up, conv_w, conv_b, w_down,
           norm_post_ffn, _n_cores=8):
    f = lambda a: np.ascontiguousarray(np.asarray(a, dtype=np.float32))
    x_prompt = f(x_prompt); x_sample = f(x_sample)
    Bp, Sp, _ = x_prompt.shape
    Bs, Ss, _ = x_sample.shape
    half = Sp // 2
    jobs_key = (("p", Sp, half + 128, half), ("s", Ss, Ss, Ss))
    nc = _get_program(jobs_key)

    identf, cbf = _consts()
    gbc = np.ascontiguousarray(np.stack([np.broadcast_to(f(g)[0][None, :], (128, D))
                                         for g in (norm_pre_mix, norm_post_mix, norm_pre_ffn, norm_post_ffn)]))
    qkn = np.ascontiguousarray(np.stack([f(q_norm_a)[0], f(k_norm_a)[0]], axis=1))
    sinkbc = np.ascontiguousarray(np.broadcast_to(f(sink_b)[0][None, :], (128, 8)))
    convb = np.ascontiguousarray(f(conv_b)[0].reshape(88, 128).T)
    cw = f(conv_w)[0]
    shared = dict(w_in=f(w_in)[0], w_ba=f(w_branch_a)[0], w_bb=f(w_branch_b)[0], w_out=f(w_out)[0],
                  w_up=f(w_up)[0], w_down=f(w_down)[0], gbc=gbc, qkn=qkn, sinkbc=sinkbc, convb=convb,
                  identf=identf, cbf=cbf)
    tab_p = {False: _rope_tables(Sp, False), True: _rope_tables(Sp, True)}
    tab_s = _rope_tables(Ss, False)
    cw_l = {False: _convw_layout(cw, False), True: _convw_layout(cw, True)}

    n_cores = _n_cores
    in_maps = []
    for c in range(n_cores):
        p, hf = c // 2, c % 2
        rev = hf == 1
        xp = x_prompt[p % Bp]
        if rev:
            xp = np.ascontiguousarray(xp[::-1])
        m = dict(shared)
        m["x_p"] = xp
        m["tab_p"] = tab_p[rev]
        m["convw_p"] = cw_l[rev]
        m["x_s"] = x_sample[c % Bs]
        m["tab_s"] = tab_s
        m["convw_s"] = cw_l[False]
        in_maps.append(m)
    res = run_bass_kernel_spmd(nc, in_maps, core_ids=list(range(n_cores)))
    y_prompt = np.zeros((Bp, Sp, D), np.float32)
    y_sample = np.zeros((Bs, Ss, D), np.float32)
    for c in range(n_cores):
        p, hf = c // 2, c % 2
        r = res.results[c]
        yp = np.asarray(r["y_p"], dtype=np.float32)
        if hf == 0:
            y_prompt[p % Bp, 0:half] = yp
        else:
            y_prompt[p % Bp, half:] = yp[::-1]
        y_sample[c % Bs] = np.asarray(r["y_s"], dtype=np.float32)
    return (y_prompt, y_sample)
```

```python
import contextlib
import numpy as np
import ml_dtypes
import concourse.bass as bass
import concourse.mybir as mybir
from concourse.bass_utils import run_bass_kernel_spmd

F32 = mybir.dt.float32
BF16 = mybir.dt.bfloat16
AF = mybir.ActivationFunctionType
ALU = mybir.AluOpType

D = 2048
NKC = 16
DFF = 5632
NFC = 44
HD = 128
EPS = 1e-6
THETA = 10000.0
GRID_W = 64
SCALE = HD ** -0.5
DSPLIT = 352


class Buf:
    __slots__ = ("name", "w", "r", "excl")

    def __init__(self, name="", excl=False):
        self.name = name
        self.w = {}
        self.r = {}
        self.excl = excl


class _Rec:
    def __init__(self):
        self.call = None

    def __getattr__(self, name):
        def f(*a, **k):
            self.call = (name, a, k)
            return self
        return f


class Prog:
    ENGS = ("pe", "act", "dve", "pool", "sp")

    def __init__(self, nc, stack, n_dma_sems=32):
        self.nc = nc
        self.ops = {e: [] for e in self.ENGS}
        self.cnt = {e: 0 for e in self.ENGS}
        self.seen = {e: {} for e in self.ENGS}
        self.n_dma_sems = n_dma_sems
        self.dma_cnt = [0] * n_dma_sems
        self.n_sw = 8
        self.dma_rr = {"sw": 0, "hw": 0}
        self.esem = {e: stack.enter_context(nc.semaphore("s_" + e)) for e in self.ENGS}
        self.dsem = [stack.enter_context(nc.semaphore("d%d" % i)) for i in range(n_dma_sems)]
        self.n_ops = 0
        self.n_wait = 0

    def _needs(self, reads, writes):
        need = {}
        reads = [b for b in reads if b is not None]
        writes = [b for b in writes if b is not None]
        for b in reads:
            for k, v in b.w.items():
                if v > need.get(k, 0):
                    need[k] = v
        for b in writes:
            for k, v in b.w.items():
                if v > need.get(k, 0):
                    need[k] = v
            for k, v in b.r.items():
                if v > need.get(k, 0):
                    need[k] = v
        return need

    def _emit_waits(self, eng, need, skip_self=False):
        seen = self.seen[eng]
        for k, v in need.items():
            if skip_self and k == eng:
                continue
            if seen.get(k, 0) >= v:
                continue
            if isinstance(k, str):
                assert v <= self.cnt[k], ("wait on not-yet-signalled op", eng, k, v, self.cnt[k])
            seen[k] = v
            self.ops[eng].append(("wait", k, v))
            self.n_wait += 1

    def _mark(self, key, val, reads, writes, more=()):
        kvs = [(key, val)] + list(more)
        reads = [b for b in reads if b is not None]
        writes = [b for b in writes if b is not None]
        for b in reads:
            for k, v in kvs:
                if b.r.get(k, 0) < v:
                    b.r[k] = v
        for b in writes:
            b.w = dict(kvs)
            b.r = {}

    def op(self, eng, fn, reads=(), writes=(), signal=True):
        skip_self = (eng == "pe")
        if any(b is not None and b.excl for b in reads):
            writes = list(writes) + [b for b in reads if b is not None and b.excl]
            reads = [b for b in reads if b is None or not b.excl]
        need = self._needs(reads, writes)
        self._emit_waits(eng, need, skip_self=skip_self)
        if signal:
            self.cnt[eng] += 1
            val = self.cnt[eng]
        else:
            val = self.cnt[eng] + 1
        rec = _Rec()
        fn(rec)
        self.ops[eng].append(("op", rec.call, signal))
        self._mark(eng, val, reads, writes)
        self.n_ops += 1

    def dma(self, q, out, in_, reads=(), writes=()):
        need = self._needs(reads, writes)
        if q == "pool":
            i = self.dma_rr["sw"]
            self.dma_rr["sw"] = (i + 1) % self.n_sw
        else:
            i = self.n_sw + self.dma_rr["hw"]
            self.dma_rr["hw"] = (self.dma_rr["hw"] + 1) % (self.n_dma_sems - self.n_sw)
        key = ("d", i)
        if self.dma_cnt[i] > 0:
            need[key] = max(need.get(key, 0), self.dma_cnt[i])
        self._emit_waits(q, need)
        self.dma_cnt[i] += 16
        self.ops[q].append(("dma", out, in_, i))
        self._mark(key, self.dma_cnt[i], reads, writes)
        self.n_ops += 1

    def dma_multi(self, q, pieces, reads=(), writes=()):
        need = self._needs(reads, writes)
        kvs = []
        for (out, in_) in pieces:
            if q == "pool":
                i = self.dma_rr["sw"]
                self.dma_rr["sw"] = (i + 1) % self.n_sw
            else:
                i = self.n_sw + self.dma_rr["hw"]
                self.dma_rr["hw"] = (self.dma_rr["hw"] + 1) % (self.n_dma_sems - self.n_sw)
            key = ("d", i)
            if self.dma_cnt[i] > 0:
                need[key] = max(need.get(key, 0), self.dma_cnt[i])
            self._emit_waits(q, need)
            need = {}
            self.dma_cnt[i] += 16
            self.ops[q].append(("dma", out, in_, i))
            kvs.append((key, self.dma_cnt[i]))
            self.n_ops += 1
        self._mark(kvs[0][0], kvs[0][1], reads, writes, more=kvs[1:])

    def barrier(self):
        for e in self.ENGS:
            need = {}
            for k in self.ENGS:
                if k != e and self.cnt[k] > 0:
                    need[k] = self.cnt[k]
            for i in range(self.n_dma_sems):
                if self.dma_cnt[i] > 0:
                    need[("d", i)] = self.dma_cnt[i]
            self._emit_waits(e, need)

    def emit(self):
        nc = self.nc

        def semof(k):
            return self.esem[k] if isinstance(k, str) else self.dsem[k[1]]

        with nc.Block() as block:
            def run(engname):
                ops = self.ops[engname]

                def body(eng):
                    for o in ops:
                        if o[0] == "wait":
                            eng.wait_ge(semof(o[1]), o[2])
                        elif o[0] == "op":
                            ins = getattr(eng, o[1][0])(*o[1][1], **o[1][2])
                            if o[2]:
                                ins.then_inc(self.esem[engname], 1)
                        else:
                            eng.dma_start(out=o[1], in_=o[2]).then_inc(self.dsem[o[3]], 16)
                return body

            block.tensor(run("pe"))
            block.scalar(run("act"))
            block.vector(run("dve"))
            block.gpsimd(run("pool"))
            block.sync(run("sp"))
        self.ops = {e: [] for e in self.ENGS}


_UID = [0]


def _uname(name):
    _UID[0] += 1
    return "%s_u%d" % (name, _UID[0])


class Ring:
    def __init__(self, stack, nc, name, shape, dt, n, psum=False):
        self.items = []
        for i in range(n):
            if psum:
                t = stack.enter_context(nc.psum_tensor(_uname(name), shape, dt))
            else:
                t = stack.enter_context(nc.sbuf_tensor(_uname(name), shape, dt))
            self.items.append((t, Buf("%s%d" % (name, i), excl=psum)))
        self.i = 0

    def next(self):
        it = self.items[self.i]
        self.i = (self.i + 1) % len(self.items)
        return it


STOP_AFTER = [99]
import os as _os
DBG = _os.environ.get("KDBG", "")


def build_program(jobs):
    nc = bass.Bass("TRN2", target_bir_lowering=False)

    def din(name, shape, dt=F32):
        return nc.dram_tensor(name, shape, dt, kind="ExternalInput").ap()

    def dscr(name, shape, dt):
        return nc.dram_tensor(name, shape, dt, kind="Internal").ap()

    w_in = din("w_in", [D, 7168])
    w_ba = din("w_ba", [1024, D])
    w_bb = din("w_bb", [1024, D])
    w_out = din("w_out", [D, D])
    w_up = din("w_up", [D, 2 * DFF])
    w_down = din("w_down", [DFF, D])
    gbc_d = din("gbc", [4, 128, D])
    qkn_d = din("qkn", [128, 2])
    sink_d = din("sinkbc", [128, 8])
    convb_d = din("convb", [128, 88])
    identf_d = din("identf", [128, 128])
    cbf_d = din("cbf", [128, 1536], BF16)

    wsrc = [(w_in, [D, 7168]), (w_ba, [1024, D]), (w_bb, [1024, D]), (w_out, [D, D]), (w_up, [D, 2 * DFF]), (w_down, [DFF, D])]
    wbf = [dscr("wbf%d" % i, shp, BF16) for i, (_, shp) in enumerate(wsrc)]
    w_in_v, w_ba_v, w_bb_v, w_out_v, w_up_v, w_down_v = [w.rearrange("(kc p) c -> p kc c", p=128) for w in wbf]

    J = []
    for jb in jobs:
        n = jb["name"]
        S, NQ, NOUT = jb["S"], jb["NQ"], jb["NOUT"]
        NQP = ((NQ + 511) // 512) * 512
        j = dict(jb)
        j["NQP"] = NQP
        j["x"] = din("x_" + n, [S, D])
        j["tab"] = din("tab_" + n, [4, 128, S])
        j["convw"] = din("convw_" + n, [128, 3, 88])
        j["y"] = nc.dram_tensor("y_" + n, [NOUT, D], F32, kind="ExternalOutput").ap()
        j["KAT"] = dscr("KAT_" + n, [2, 128, S], BF16)
        j["KBT"] = dscr("KBT_" + n, [2, 128, S], BF16)
        j["VA"] = dscr("VA_" + n, [S, 256], BF16)
        j["VB"] = dscr("VB_" + n, [S, 256], BF16)
        j["QAT"] = dscr("QAT_" + n, [8, 128, NQP], BF16)
        j["QBT"] = dscr("QBT_" + n, [8, 128, NQP], BF16)
        j["OAT"] = dscr("OAT_" + n, [8, 128, NQP], BF16)
        j["OBT"] = dscr("OBT_" + n, [8, 128, NQP], BF16)
        j["X1"] = dscr("X1_" + n, [NQP, D], F32)
        j["H2T"] = dscr("H2T_" + n, [16, 128, NQP], BF16)
        j["B"] = {k: None for k in ("KAT", "KBT", "VA", "VB", "QAT", "QBT", "OAT", "OBT", "X1", "H2T")}
        J.append(j)
    SMAX = max(j["S"] for j in J)

    top = contextlib.ExitStack()
    with top:
        P = Prog(nc, top)

        def sb(stack, name, shape, dt=F32):
            return stack.enter_context(nc.sbuf_tensor(_uname(name), shape, dt))

        identf = sb(top, "identf", [128, 128]); B_identf = Buf()
        cbf = sb(top, "cbf", [128, 1536], BF16); B_cbf = Buf()
        onesf = sb(top, "onesf", [128, 128]); B_onesf = Buf()
        qkn = sb(top, "qkn", [128, 2]); B_qkn = Buf()
        epsb = sb(top, "epsb", [128, 1]); B_epsb = Buf()
        expsink = sb(top, "expsink", [128, 8]); B_expsink = Buf()
        convb = sb(top, "convb", [128, 88]); B_convb = Buf()
        identb = cbf[:, 0:128]
        onesb = cbf[:, 128:256]
        rotA = cbf[:, 256:384]
        rotB = cbf[:, 384:512]
        maskP = cbf[:, 512:1024]
        maskN = cbf[:, 1024:1536]
        P.dma("sp", identf[:], identf_d[:, :], writes=[B_identf])
        P.dma("sp", cbf[:], cbf_d[:, :], writes=[B_cbf])
        P.dma("sp", qkn[:], qkn_d[:, :], writes=[B_qkn])
        P.dma("sp", expsink[:], sink_d[:, :], writes=[B_expsink])
        P.dma("sp", convb[:], convb_d[:, :], writes=[B_convb])
        P.op("dve", lambda e: e.memset(onesf[:], 1.0), writes=[B_onesf])
        P.op("dve", lambda e: e.memset(epsb[:], EPS), writes=[B_epsb])
        P.op("act", lambda e: e.activation(out=expsink[:], in_=expsink[:], func=AF.Exp),
             reads=[B_expsink], writes=[B_expsink])

        late_casts = []
        for wi, ((wf, (K_, C_)), wb_) in enumerate(zip(wsrc, wbf)):
            for r0 in range(0, K_, 128):
                for c0 in range(0, C_, 2048):
                    cw_ = min(2048, C_ - c0)
                    if wi == 0:
                        P.dma("pool", wb_[r0:r0 + 128, c0:c0 + cw_], wf[r0:r0 + 128, c0:c0 + cw_])
                    else:
                        late_casts.append((wb_[r0:r0 + 128, c0:c0 + cw_], wf[r0:r0 + 128, c0:c0 + cw_]))
        P.barrier()

        def issue_late_casts(n):
            for _ in range(n):
                if late_casts:
                    d_, s_ = late_casts.pop(0)
                    P.dma("pool", d_, s_)
        if DBG == "T1":
            P.emit()
            return nc

        def mm(out, lhsT, rhs, start, stop, reads, writes, signal):
            P.op("pe", lambda e: e.matmul(out, lhsT=lhsT, rhs=rhs, start=start, stop=stop),
                 reads=reads, writes=writes, signal=signal)

        def mm_group(out, pairs, reads, writes):
            n = len(pairs)
            for i, (l, r) in enumerate(pairs):
                mm(out, l, r, i == 0, i == n - 1, reads, writes, i == n - 1)

        def rstd_from(ss_ap, B_ss, n, nsz=128):
            P.op("act", lambda e: e.activation(out=ss_ap, in_=ss_ap, func=AF.Sqrt, scale=1.0 / n, bias=epsb[0:nsz, 0:1]),
                 reads=[B_ss, B_epsb], writes=[B_ss])
            P.op("dve", lambda e: e.reciprocal(out=ss_ap, in_=ss_ap), reads=[B_ss], writes=[B_ss])

        class Pipe:
            def __init__(self, depth=1):
                self.q = []
                self.depth = depth

            def push(self, fn):
                self.q.append(fn)
                while len(self.q) > self.depth:
                    self.q.pop(0)()

            def flush(self):
                while self.q:
                    self.q.pop(0)()

        def norm_part(R, xs, B_xs, g_ap, B_g, nsz=128):
            ss, B_ss = R["small"].next()
            jk, B_jk = R["junk"].next()
            P.op("act", lambda e: e.activation(out=jk[0:nsz, :], in_=xs, func=AF.Square, accum_out=ss[0:nsz, 0:1]),
                 reads=[B_xs], writes=[B_jk, B_ss])
            rstd_from(ss[0:nsz, 0:1], B_ss, D, nsz)
            hb, B_hb = R["hb"].next()
            P.op("dve", lambda e: e.scalar_tensor_tensor(out=hb[0:nsz, :], in0=xs, scalar=ss[0:nsz, 0:1], in1=g_ap,
                                                          op0=ALU.mult, op1=ALU.mult),
                 reads=[B_xs, B_ss, B_g], writes=[B_hb])
            return hb, B_hb

        def transpose_part(R, hb, B_hb, hT, B_hT, col0, nsz=128):
            for q in range(4):
                tb, B_tb = R["pb"].next()
                for i in range(4):
                    kc = 4 * q + i
                    P.op("pe", lambda e, i=i, kc=kc, tb=tb: e.transpose(tb[:, i, 0:nsz], hb[0:nsz, kc * 128:(kc + 1) * 128], identb[0:nsz, 0:nsz]),
                         reads=[B_hb, B_cbf], writes=[B_tb], signal=(i == 3))
                P.op("act", lambda e, q=q, tb=tb: e.activation(out=hT[:, 4 * q:4 * q + 4, col0:col0 + nsz], in_=tb[:, 0:4, 0:nsz], func=AF.Copy),
                     reads=[B_tb], writes=[B_hT])

        def norm_transpose(R, xs, B_xs, g_ap, B_g, hT, B_hT, col0):
            hb, B_hb = norm_part(R, xs, B_xs, g_ap, B_g)
            transpose_part(R, hb, B_hb, hT, B_hT, col0)

        def load_x_block(R, xap, t0, nsub):
            hs = []
            for s in range(nsub):
                xs, B_xs = R["xst"].next()
                r0 = t0 + s * 128
                P.dma_multi("sp", [(xs[:, c_:c_ + 512], xap[r0:r0 + 128, c_:c_ + 512]) for c_ in range(0, D, 512)], writes=[B_xs])
                hs.append((xs, B_xs))
            return hs

        def subtiles(nt):
            return [(s, min(128, nt - s * 128)) for s in range((nt + 127) // 128)]

        def to_token_major(R, src_bank, B_src, Rt, B_Rt, c, nt):
            mst, B_mst = R["mst"].next()
            P.op("act", lambda e: e.activation(out=mst[:, 0:nt], in_=src_bank[:, 0:nt], func=AF.Copy),
                 reads=[B_src], writes=[B_mst])
            tb, B_tb = R["pf"].next()
            tbv = tb[:].rearrange("p (a b) -> p a b", a=4)
            st = subtiles(nt)
            for (s, nsz) in st:
                P.op("pe", lambda e, s=s, nsz=nsz: e.transpose(tbv[0:nsz, s, :], mst[:, s * 128:s * 128 + nsz], identf[:]),
                     reads=[B_mst, B_identf], writes=[B_tb], signal=(s == st[-1][0]))
            nfull = nt // 128
            if nfull:
                P.op("dve", lambda e: e.tensor_copy(out=Rt[:, 0:nfull, c * 128:(c + 1) * 128], in_=tbv[:, 0:nfull, :]),
                     reads=[B_tb], writes=[B_Rt])
            if nt % 128:
                nsz = nt % 128
                P.op("dve", lambda e: e.tensor_copy(out=Rt[0:nsz, nfull, c * 128:(c + 1) * 128], in_=tbv[0:nsz, nfull, :]),
                     reads=[B_tb], writes=[B_Rt])

        def resid_norm(R, Rt, B_Rt, s, xres, B_xres, g_ap, B_g, nsz=128):
            ss, B_ss = R["small"].next()
            jk, B_jk = R["junk"].next()
            P.op("act", lambda e: e.activation(out=jk[0:nsz, :], in_=Rt[0:nsz, s, :], func=AF.Square, accum_out=ss[0:nsz, 0:1]),
                 reads=[B_Rt], writes=[B_jk, B_ss])
            rstd_from(ss[0:nsz, 0:1], B_ss, D, nsz)
            P.op("dve", lambda e: e.scalar_tensor_tensor(out=Rt[0:nsz, s, :], in0=Rt[0:nsz, s, :], scalar=ss[0:nsz, 0:1], in1=g_ap,
                                                          op0=ALU.mult, op1=ALU.mult),
                 reads=[B_Rt, B_ss, B_g], writes=[B_Rt])
            P.op("dve", lambda e: e.tensor_tensor(out=Rt[0:nsz, s, :], in0=Rt[0:nsz, s, :], in1=xres, op=ALU.add),
                 reads=[B_Rt, B_xres], writes=[B_Rt])

        with contextlib.ExitStack() as ps:
            R = {}
            R["xst"] = Ring(ps, nc, "xst", [128, D], F32, 4)
            R["junk"] = Ring(ps, nc, "junk", [128, D], BF16, 1)
            R["hb"] = Ring(ps, nc, "hb", [128, D], BF16, 4)
            R["small"] = Ring(ps, nc, "small", [128, 1], F32, 8)
            R["hT"] = Ring(ps, nc, "hT", [128, NKC, 512], BF16, 2)
            R["w"] = Ring(ps, nc, "wsl", [128, NKC, 256], BF16, 4)
            R["tab"] = Ring(ps, nc, "tab", [128, 4, 512], F32, 2)
            R["tf"] = Ring(ps, nc, "tf", [128, 512], F32, 6)
            R["tb"] = Ring(ps, nc, "tb", [128, 512], BF16, 4)
            R["ob"] = Ring(ps, nc, "ob", [128, 512], BF16, 4)
            R["vst"] = Ring(ps, nc, "vst", [128, 256], BF16, 4)
            R["pf"] = Ring(ps, nc, "pf", [128, 512], F32, 6, psum=True)
            R["pb"] = Ring(ps, nc, "pb", [128, 8, 128], BF16, 2, psum=True)
            gbc0 = sb(ps, "gbc0", [128, D]); B_g0 = Buf()
            P.dma("sp", gbc0[:], gbc_d[0, :, :], writes=[B_g0])

            for j in J:
                S, NQ = j["S"], j["NQ"]
                JB = j["B"]
                blocks = [(t0, 512) for t0 in range(0, S, 512)]
                xl = load_x_block(R, j["x"], 0, 4)
                hbs = [norm_part(R, xl[s][0][:], xl[s][1], gbc0[:], B_g0) for s in range(4)]
                nxt_hT = R["hT"].next()
                for s in range(4):
                    transpose_part(R, hbs[s][0], hbs[s][1], nxt_hT[0], nxt_hT[1], s * 128)
                for bi, (t0, nt) in enumerate(blocks):
                    own = t0 < NQ
                    hT, B_hT = nxt_hT
                    has_next = bi + 1 < len(blocks)
                    tab, B_tab = R["tab"].next()
                    P.dma("sp", tab[:], j["tab"][:, :, t0:t0 + 512].rearrange("f p n -> p f n"), writes=[B_tab])

                    slabs = []
                    if own:
                        slabs += [("qA", c0) for c0 in (0, 256, 512, 768)]
                    slabs += [("kA", 1024), ("vA", 1280)]
                    if own:
                        slabs += [("qB", c0) for c0 in (1536, 1792, 2048, 2304)]
                    slabs += [("kB", 2560), ("vB", 2816)]
                    pipe = Pipe()
                    vpos = [i_ for i_, (k_, _) in enumerate(slabs) if k_ == "vA"][0]
                    if vpos >= 4:
                        ldsched = {vpos - 4: [0], vpos - 3: [1], vpos - 2: [2], vpos - 1: [3]}
                    else:
                        ldsched = {0: [0, 1], vpos: [2, 3]}
                    if has_next:
                        hbs = [None] * 4
                        xls = [None] * 4
                    for si, (kind, c0) in enumerate(slabs):
                        if has_next and si in ldsched:
                            for s_ in ldsched[si]:
                                (xls[s_],) = load_x_block(R, j["x"], blocks[bi + 1][0] + s_ * 128, 1)
                        if has_next and si == vpos:
                            for s_ in range(4):
                                hbs[s_] = norm_part(R, xls[s_][0][:], xls[s_][1], gbc0[:], B_g0)
                        if has_next and si == len(slabs) - 2:
                            nxt_hT = R["hT"].next()
                            for s in range(4):
                                transpose_part(R, hbs[s][0], hbs[s][1], nxt_hT[0], nxt_hT[1], s * 128)
                        wt, B_wt = R["w"].next()
                        P.dma("pool", wt[:], w_in_v[:, :, c0:c0 + 256], writes=[B_wt])
                        if kind[0] == "v":
                            vd, B_vd = (j["VA"], JB["VA"]) if kind == "vA" else (j["VB"], JB["VB"])
                            for s in range(4):
                                bk, B_bk = R["pf"].next()
                                mm_group(bk[:, 0:256], [(hT[:, kc, s * 128:(s + 1) * 128], wt[:, kc, :]) for kc in range(NKC)],
                                         [B_hT, B_wt], [B_bk])

                                def vpost(bk=bk, B_bk=B_bk, s=s, vd=vd, B_vd=B_vd):
                                    vs, B_vs = R["vst"].next()
                                    P.op("act", lambda e: e.activation(out=vs[:], in_=bk[:, 0:256], func=AF.Copy),
                                         reads=[B_bk], writes=[B_vs])
                                    r0 = t0 + s * 128
                                    P.dma("sp", vd[r0:r0 + 128, :], vs[:], reads=[B_vs], writes=[B_vd])
                                if DBG not in ("T3", "T5"):
                                    pipe.push(vpost)
                            continue
                        for half in range(2):
                            bk, B_bk = R["pf"].next()
                            mm_group(bk[:], [(wt[:, kc, half * 128:(half + 1) * 128], hT[:, kc, :]) for kc in range(NKC)],
                                     [B_hT, B_wt], [B_bk])
                            if kind == "qA":
                                idx = c0 // 128 + half
                                dst, B_dst = j["QAT"][idx, :, t0:t0 + 512], JB["QAT"]
                            elif kind == "qB":
                                idx = (c0 - 1536) // 128 + half
                                dst, B_dst = j["QBT"][idx, :, t0:t0 + 512], JB["QBT"]
                            elif kind == "kA":
                                dst, B_dst = j["KAT"][half, :, t0:t0 + 512], JB["KAT"]
                            else:
                                dst, B_dst = j["KBT"][half, :, t0:t0 + 512], JB["KBT"]
                            isA = kind[1] == "A"

                            def post(bk=bk, B_bk=B_bk, isA=isA, kind=kind, dst=dst, B_dst=B_dst, tab=tab, B_tab=B_tab):
                                qg, B_qg = R["tb"].next()
                                t1, B_t1 = R["tf"].next()
                                t2, B_t2 = R["tf"].next()
                                ob, B_ob = R["ob"].next()
                                b3, B_b3 = R["pf"].next()
                                if isA:
                                    sq, B_sq = R["tf"].next()
                                    col = 0 if kind[0] == "q" else 1
                                    P.op("act", lambda e: e.activation(out=sq[:], in_=bk[:], func=AF.Square),
                                         reads=[B_bk], writes=[B_sq])
                                    P.op("act", lambda e: e.activation(out=qg[:], in_=bk[:], func=AF.Copy, scale=qkn[:, col:col + 1]),
                                         reads=[B_bk, B_qkn], writes=[B_qg])
                                    b2, B_b2 = R["pf"].next()
                                    mm(b2[:], onesf[:], sq[:], True, True, [B_onesf, B_sq], [B_b2], True)
                                    mm(b3[:], rotA, qg[:], True, True, [B_cbf, B_qg], [B_b3], True)
                                    rs, B_rs = sq, B_sq
                                    P.op("act", lambda e: e.activation(out=rs[:], in_=b2[:], func=AF.Ln, scale=1.0 / HD, bias=epsb[:, 0:1]),
                                         reads=[B_b2, B_epsb], writes=[B_rs])
                                    P.op("act", lambda e: e.activation(out=rs[:], in_=rs[:], func=AF.Exp, scale=-0.5), reads=[B_rs], writes=[B_rs])
                                    P.op("dve", lambda e: e.tensor_tensor(out=t1[:], in0=qg[:], in1=tab[:, 0, :], op=ALU.mult),
                                         reads=[B_qg, B_tab], writes=[B_t1])
                                    P.op("dve", lambda e: e.tensor_tensor(out=t2[:], in0=b3[:], in1=tab[:, 1, :], op=ALU.mult),
                                         reads=[B_b3, B_tab], writes=[B_t2])
                                    P.op("dve", lambda e: e.tensor_tensor(out=t1[:], in0=t1[:], in1=t2[:], op=ALU.add),
                                         reads=[B_t1, B_t2], writes=[B_t1])
                                    P.op("dve", lambda e: e.tensor_tensor(out=ob[:], in0=t1[:], in1=rs[:], op=ALU.mult),
                                         reads=[B_t1, B_rs], writes=[B_ob])
                                else:
                                    P.op("act", lambda e: e.activation(out=qg[:], in_=bk[:], func=AF.Copy),
                                         reads=[B_bk], writes=[B_qg])
                                    mm(b3[:], rotB, qg[:], True, True, [B_cbf, B_qg], [B_b3], True)
                                    P.op("dve", lambda e: e.tensor_tensor(out=t1[:], in0=bk[:], in1=tab[:, 2, :], op=ALU.mult),
                                         reads=[B_bk, B_tab], writes=[B_t1])
                                    P.op("dve", lambda e: e.tensor_tensor(out=t2[:], in0=b3[:], in1=tab[:, 3, :], op=ALU.mult),
                                         reads=[B_b3, B_tab], writes=[B_t2])
                                    P.op("dve", lambda e: e.tensor_tensor(out=ob[:], in0=t1[:], in1=t2[:], op=ALU.add),
                                         reads=[B_t1, B_t2], writes=[B_ob])
                                P.dma("sp", dst, ob[:], reads=[B_ob], writes=[B_dst])
                            if DBG not in ("T3", "T4"):
                                pipe.push(post)
                    pipe.flush()
            P.barrier()
            P.emit()

        if STOP_AFTER[0] < 2:
            return nc
        with contextlib.ExitStack() as ps:
            R = {}
            NKB = SMAX // 128
            R["K"] = Ring(ps, nc, "Kt", [128, SMAX], BF16, 2)
            R["V"] = Ring(ps, nc, "Vt", [128, NKB, 128], BF16, 2)
            R["Q"] = Ring(ps, nc, "Qt", [128, 4, 512], BF16, 2)
            R["QB"] = Ring(ps, nc, "QBt", [128, 8, 512], BF16, 2)
            R["obst"] = Ring(ps, nc, "obst", [128, 8, 512], BF16, 1)
            R["pt"] = Ring(ps, nc, "pt", [128, 2, 512], BF16, 6)
            R["ptB"] = Ring(ps, nc, "ptB", [128, 512], BF16, 6)
            R["accD"] = Ring(ps, nc, "accD", [128, 2, DSPLIT], F32, 2)
            R["accP"] = Ring(ps, nc, "accP", [128, 2, 512 - DSPLIT], F32, 2)
            R["tf"] = Ring(ps, nc, "tf", [128, 512], F32, 3)
            R["ob"] = Ring(ps, nc, "ob", [128, 512], BF16, 2)
            R["pS2"] = Ring(ps, nc, "pS2", [128, 2, 512], F32, 2, psum=True)
            R["pO"] = Ring(ps, nc, "pO", [128, 512], F32, 2, psum=True)
            R["pM"] = Ring(ps, nc, "pM", [128, 512], F32, 2, psum=True)
            KB0 = sb(ps, "KB0", [128, SMAX], BF16); KB1 = sb(ps, "KB1", [128, SMAX], BF16)
            VBt = sb(ps, "VBt", [128, NKB, 256], BF16)
            B_KB = [Buf(), Buf()]; B_VBt = Buf()
            sinkrow = sb(ps, "sinkrow", [128, 2, 512]); B_sinkrow = Buf()
            zer = sb(ps, "zer", [128, 128]); B_zer = Buf()
            P.op("dve", lambda e: e.memset(zer[:], 0.0), writes=[B_zer])
            for h in range(8):
                P.op("act", lambda e, h=h: e.activation(out=sinkrow[:, h // 4, (h % 4) * 128:(h % 4 + 1) * 128], in_=zer[:],
                                                        func=AF.Identity, bias=expsink[:, h:h + 1]),
                     reads=[B_zer, B_expsink], writes=[B_sinkrow])
            KBs = [KB0, KB1]

            def make_bunit(j, t0, nt, sj, g, first, lastu, blk, nkb):
                qbi = t0 // 128 + sj
                kbs = [k for k in (qbi - 1, qbi, qbi + 1) if 0 <= k < nkb]
                st = {}

                def s1():
                    if first:
                        blk["QB"] = R["QB"].next()
                        blk["obst"] = R["obst"].next()
                        QB, B_QB = blk["QB"]
                        P.dma("sp", QB[:, :, 0:nt], j["QBT"][:, :, t0:t0 + nt].rearrange("h p n -> p h n"), writes=[B_QB])
                    QB, B_QB = blk["QB"]
                    pts = []
                    for k in kbs:
                        sbk2, B_sbk = R["pS2"].next()
                        sbk = sbk2[:, 0, :]
                        sbv = sbk.rearrange("p (a b) -> p a b", a=4)
                        mm(sbv, KBs[g][:, k * 128:(k + 1) * 128], QB[:, 4 * g:4 * g + 4, sj * 128:(sj + 1) * 128], True, True,
                           [B_KB[g], B_QB], [B_sbk], True)
                        pt, B_pt = R["ptB"].next()
                        P.op("act", lambda e, pt=pt, sbk=sbk: e.activation(out=pt[:], in_=sbk, func=AF.Exp, scale=SCALE),
                             reads=[B_sbk], writes=[B_pt])
                        if k != qbi:
                            msk = maskP if k < qbi else maskN
                            P.op("dve", lambda e, pt=pt, msk=msk: e.tensor_tensor(out=pt[:], in0=pt[:], in1=msk, op=ALU.mult),
                                 reads=[B_pt, B_cbf], writes=[B_pt])
                        pts.append((k, pt, B_pt))
                    st["pts"] = pts

                def s2():
                    pts = st["pts"]
                    obst, B_obst = blk["obst"]
                    O, B_O = R["pO"].next()
                    M, B_M = R["pM"].next()
                    n = len(pts)
                    for ii, (k, pt, B_pt) in enumerate(pts):
                        last = ii == n - 1
                        mm(O[:], VBt[:, k, g * 128:(g + 1) * 128], pt[:], ii == 0, last, [B_VBt, B_pt], [B_O], last)
                        mm(M[:], onesb, pt[:], ii == 0, last, [B_cbf, B_pt], [B_M], last)
                    rc, B_rc = R["tf"].next()
                    P.op("dve", lambda e: e.tensor_tensor(out=rc[:], in0=M[:], in1=sinkrow[:, g, :], op=ALU.add),
                         reads=[B_M, B_sinkrow], writes=[B_rc])
                    P.op("act", lambda e: e.activation(out=rc[:], in_=rc[:], func=AF.Ln), reads=[B_rc], writes=[B_rc])
                    P.op("act", lambda e: e.activation(out=rc[:], in_=rc[:], func=AF.Exp, scale=-1.0), reads=[B_rc], writes=[B_rc])
                    P.op("dve", lambda e: e.tensor_tensor(
                        out=obst[:, 4 * g:4 * g + 4, sj * 128:(sj + 1) * 128],
                        in0=O[:].rearrange("p (a b) -> p a b", a=4),
                        in1=rc[:].rearrange("p (a b) -> p a b", a=4), op=ALU.mult),
                         reads=[B_O, B_rc], writes=[B_obst])
                    if lastu:
                        P.dma("sp", j["OBT"][:, :, t0:t0 + nt].rearrange("h p n -> p h n"), obst[:, :, 0:nt], reads=[B_obst])
                return s1, s2

            for j in J:
                S, NQ = j["S"], j["NQ"]
                nkb = S // 128
                qblocks = [(t0, min(512, NQ - t0)) for t0 in range(0, NQ, 512)]
                for g in range(2):
                    P.dma("sp", KBs[g][:, 0:S], j["KBT"][g, :, :], writes=[B_KB[g]])
                P.dma("sp", VBt[:, 0:nkb, :], j["VB"].rearrange("(kb p) c -> p kb c", p=128), writes=[B_VBt])
                bunits = []
                for (t0, nt) in qblocks:
                    nsub = nt // 128
                    blk = {}
                    for sj in range(nsub):
                        for g in range(2):
                            bunits.append(make_bunit(j, t0, nt, sj, g, sj == 0 and g == 0, sj == nsub - 1 and g == 1, blk, nkb))
                ui = 0
                for g in range(2):
                    Kt, B_Kt = R["K"].next()
                    Vt, B_Vt = R["V"].next()
                    P.dma("sp", Kt[:, 0:S], j["KAT"][g, :, :], writes=[B_Kt])
                    P.dma("sp", Vt[:, 0:nkb, :], j["VA"][:, g * 128:(g + 1) * 128].rearrange("(kb p) d -> p kb d", p=128), writes=[B_Vt])
                    for (t0, nt) in qblocks:
                        Qt, B_Qt = R["Q"].next()
                        P.dma("sp", Qt[:, :, 0:nt], j["QAT"][4 * g:4 * g + 4, :, t0:t0 + nt].rearrange("h p n -> p h n"), writes=[B_Qt])
                        for hh in range(4):
                            bu = bunits[ui] if ui < len(bunits) else None
                            ui += 1
                            issue_late_casts(2)
                            if bu is not None:
                                bu[0]()
                            O, B_O = R["pO"].next()
                            accD, B_accD = R["accD"].next()
                            accP, B_accP = R["accP"].next()
                            dsp = min(DSPLIT, nt)
                            pipe = Pipe(3)
                            assert nkb % 2 == 0
                            for kp in range(nkb // 2):
                                sb2, B_sb2 = R["pS2"].next()
                                for u in range(2):
                                    kb = 2 * kp + u
                                    mm(sb2[:, u, 0:nt], Kt[:, kb * 128:(kb + 1) * 128], Qt[:, hh, 0:nt], True, True,
                                       [B_Kt, B_Qt], [B_sb2], u == 1)
                                pt, B_pt = R["pt"].next()
                                P.op("act", lambda e: e.activation(out=pt[:, :, 0:nt], in_=sb2[:, :, 0:nt], func=AF.Exp, scale=SCALE),
                                     reads=[B_sb2], writes=[B_pt])
                                if kp == 0:
                                    P.op("dve", lambda e: e.tensor_copy(out=accD[:, :, 0:dsp], in_=pt[:, :, 0:dsp]), reads=[B_pt], writes=[B_accD])
                                    if nt > dsp:
                                        P.op("pool", lambda e: e.tensor_copy(out=accP[:, :, 0:nt - dsp], in_=pt[:, :, dsp:nt]), reads=[B_pt], writes=[B_accP])
                                else:
                                    P.op("dve", lambda e: e.tensor_tensor(out=accD[:, :, 0:dsp], in0=accD[:, :, 0:dsp], in1=pt[:, :, 0:dsp], op=ALU.add),
                                         reads=[B_pt, B_accD], writes=[B_accD])
                                    if nt > dsp:
                                        P.op("pool", lambda e: e.tensor_tensor(out=accP[:, :, 0:nt - dsp], in0=accP[:, :, 0:nt - dsp], in1=pt[:, :, dsp:nt], op=ALU.add),
                                             reads=[B_pt, B_accP], writes=[B_accP])

                                def pv(kp=kp, pt=pt, B_pt=B_pt, O=O, B_O=B_O):
                                    for u in range(2):
                                        kb = 2 * kp + u
                                        last = kb == nkb - 1
                                        mm(O[:, 0:nt], Vt[:, kb, :], pt[:, u, 0:nt], kb == 0, last, [B_Vt, B_pt], [B_O], last)
                                pipe.push(pv)
                            pipe.flush()
                            M, B_M = R["pM"].next()
                            mm(M[:, 0:dsp], onesf[:], accD[:, 0, 0:dsp], True, False, [B_onesf, B_accD], [B_M], False)
                            mm(M[:, 0:dsp], onesf[:], accD[:, 1, 0:dsp], False, True, [B_onesf, B_accD], [B_M], True)
                            if nt > dsp:
                                mm(M[:, dsp:nt], onesf[:], accP[:, 0, 0:nt - dsp], True, False, [B_onesf, B_accP], [B_M], False)
                                mm(M[:, dsp:nt], onesf[:], accP[:, 1, 0:nt - dsp], False, True, [B_onesf, B_accP], [B_M], True)
                            rc, B_rc = R["tf"].next()
                            ob, B_ob = R["ob"].next()
                            P.op("act", lambda e, rc=rc, M=M: e.activation(out=rc[:, 0:nt], in_=M[:, 0:nt], func=AF.Ln), reads=[B_M], writes=[B_rc])
                            P.op("act", lambda e, rc=rc: e.activation(out=rc[:, 0:nt], in_=rc[:, 0:nt], func=AF.Exp, scale=-1.0), reads=[B_rc], writes=[B_rc])
                            P.op("dve", lambda e, rc=rc, O=O, ob=ob: e.tensor_tensor(out=ob[:, 0:nt], in0=O[:, 0:nt], in1=rc[:, 0:nt], op=ALU.mult),
                                 reads=[B_O, B_rc], writes=[B_ob])
                            P.dma("sp", j["OAT"][4 * g + hh, :, t0:t0 + nt], ob[:, 0:nt], reads=[B_ob])
                            if bu is not None:
                                bu[1]()
                while ui < len(bunits):
                    bunits[ui][0]()
                    bunits[ui][1]()
                    ui += 1
            issue_late_casts(len(late_casts))
            P.barrier()
            P.emit()

        if STOP_AFTER[0] < 3:
            return nc
        with contextlib.ExitStack() as ps:
            R = {}
            R["xst"] = Ring(ps, nc, "xst", [128, D], F32, 2)
            R["junk"] = Ring(ps, nc, "junk", [128, D], BF16, 1)
            R["hb"] = Ring(ps, nc, "hb", [128, D], BF16, 3)
            R["small"] = Ring(ps, nc, "small", [128, 1], F32, 8)
            R["w"] = Ring(ps, nc, "wsl", [128, NKC, 256], BF16, 3)
            R["wb"] = Ring(ps, nc, "wbr", [128, 8, 256], BF16, 3)
            R["tf"] = Ring(ps, nc, "tf", [128, 512], F32, 4)
            R["mst"] = Ring(ps, nc, "mst", [128, 512], F32, 2)
            R["pf"] = Ring(ps, nc, "pf", [128, 512], F32, 6, psum=True)
            R["pb"] = Ring(ps, nc, "pb", [128, 8, 128], BF16, 2, psum=True)
            hT = sb(ps, "hT3", [128, NKC, 512], BF16); B_hT = Buf()
            h2st = sb(ps, "h2st", [128, NKC, 512], BF16); B_h2st = Buf()
            oA = sb(ps, "oA", [128, 8, 512], BF16); B_oA = Buf()
            oB = sb(ps, "oB", [128, 8, 512], BF16); B_oB = Buf()
            mg = sb(ps, "mg", [128, NKC, 512], BF16); B_mg = Buf()
            Rt = sb(ps, "Rt", [128, 4, D]); B_Rt = Buf()
            g0 = sb(ps, "g0", [128, D]); g1 = sb(ps, "g1", [128, D]); g2 = sb(ps, "g2", [128, D])
            B_gs = [Buf(), Buf(), Buf()]
            for i, gt in enumerate((g0, g1, g2)):
                P.dma("sp", gt[:], gbc_d[i, :, :], writes=[B_gs[i]])

            blocks3 = [(j, t0, min(512, j["NQ"] - t0)) for j in J for t0 in range(0, j["NQ"], 512)]

            def head_a(blk, s_):
                j, t0, nt = blk
                (xs_, B_xs_), = load_x_block(R, j["x"], t0 + s_ * 128, 1)
                return norm_part(R, xs_[:], B_xs_, g0[:], B_gs[0])

            def head_b(blk, s_, hb_):
                transpose_part(R, hb_[0], hb_[1], hT, B_hT, s_ * 128)

            def head_o(blk):
                j, t0, nt = blk
                P.dma("sp", oA[:, :, 0:nt], j["OAT"][:, :, t0:t0 + nt].rearrange("h p n -> p h n"), writes=[B_oA])
                P.dma("sp", oB[:, :, 0:nt], j["OBT"][:, :, t0:t0 + nt].rearrange("h p n -> p h n"), writes=[B_oB])

            def epi_a(blk, s_):
                j, t0, nt = blk
                (xs_, B_xs_), = load_x_block(R, j["x"], t0 + s_ * 128, 1)
                resid_norm(R, Rt, B_Rt, s_, xs_[:], B_xs_, g1[:], B_gs[1])
                r0 = t0 + s_ * 128
                P.dma_multi("sp", [(j["X1"][r0:r0 + 128, c_:c_ + 512], Rt[:, s_, c_:c_ + 512]) for c_ in range(0, D, 512)], reads=[B_Rt])
                return norm_part(R, Rt[:, s_, :], B_Rt, g2[:], B_gs[2])

            def epi_b(blk, s_, hb_):
                j, t0, nt = blk
                transpose_part(R, hb_[0], hb_[1], h2st, B_h2st, s_ * 128)
                if s_ == nt // 128 - 1:
                    P.dma("sp", j["H2T"][:, :, t0:t0 + nt].rearrange("k p n -> p k n"), h2st[:, :, 0:nt], reads=[B_h2st])

            for s_ in range(blocks3[0][2] // 128):
                head_b(blocks3[0], s_, head_a(blocks3[0], s_))
            head_o(blocks3[0])

            for bi, blk in enumerate(blocks3):
                j, t0, nt = blk
                nsub = nt // 128
                prev = blocks3[bi - 1] if bi > 0 else None
                nxt = blocks3[bi + 1] if bi + 1 < len(blocks3) else None
                ehb = {}
                pipe = Pipe()
                for cp in range(8):
                    if prev is not None:
                        pn = prev[2] // 128
                        if 0 <= cp - 2 < pn:
                            epi_b(prev, cp - 2, ehb[cp - 2])
                        if cp < pn:
                            ehb[cp] = epi_a(prev, cp)
                    wga, B_wga = R["w"].next()
                    P.dma("pool", wga[:], w_in_v[:, :, 3072 + cp * 256:3072 + (cp + 1) * 256], writes=[B_wga])
                    wgb, B_wgb = R["w"].next()
                    P.dma("pool", wgb[:], w_in_v[:, :, 5120 + cp * 256:5120 + (cp + 1) * 256], writes=[B_wgb])
                    wba, B_wba = R["wb"].next()
                    P.dma("pool", wba[:], w_ba_v[:, :, cp * 256:(cp + 1) * 256], writes=[B_wba])
                    wbb, B_wbb = R["wb"].next()
                    P.dma("pool", wbb[:], w_bb_v[:, :, cp * 256:(cp + 1) * 256], writes=[B_wbb])
                    for half in range(2):
                        c = 2 * cp + half
                        cs = slice(half * 128, (half + 1) * 128)
                        bga, B_bga = R["pf"].next()
                        mm_group(bga[:, 0:nt], [(wga[:, kc, cs], hT[:, kc, 0:nt]) for kc in range(NKC)], [B_wga, B_hT], [B_bga])
                        bgb, B_bgb = R["pf"].next()
                        mm_group(bgb[:, 0:nt], [(wgb[:, kc, cs], hT[:, kc, 0:nt]) for kc in range(NKC)], [B_wgb, B_hT], [B_bgb])
                        sa, B_sa = R["tf"].next()
                        sbb, B_sbb = R["tf"].next()

                        def post_gate(bga=bga, B_bga=B_bga, bgb=bgb, B_bgb=B_bgb, sa=sa, B_sa=B_sa, sbb=sbb, B_sbb=B_sbb, nt=nt):
                            P.op("act", lambda e: e.activation(out=sa[:, 0:nt], in_=bga[:, 0:nt], func=AF.Sigmoid), reads=[B_bga], writes=[B_sa])
                            P.op("act", lambda e: e.activation(out=sbb[:, 0:nt], in_=bgb[:, 0:nt], func=AF.Sigmoid), reads=[B_bgb], writes=[B_sbb])
                        pipe.push(post_gate)
                        bba, B_bba = R["pf"].next()
                        mm_group(bba[:, 0:nt], [(wba[:, kc, cs], oA[:, kc, 0:nt]) for kc in range(8)], [B_wba, B_oA], [B_bba])
                        bbb, B_bbb = R["pf"].next()
                        mm_group(bbb[:, 0:nt], [(wbb[:, kc, cs], oB[:, kc, 0:nt]) for kc in range(8)], [B_wbb, B_oB], [B_bbb])

                        def post_br(c=c, bba=bba, B_bba=B_bba, bbb=bbb, B_bbb=B_bbb, sa=sa, B_sa=B_sa, sbb=sbb, B_sbb=B_sbb, nt=nt):
                            P.op("dve", lambda e: e.tensor_tensor(out=sa[:, 0:nt], in0=bba[:, 0:nt], in1=sa[:, 0:nt], op=ALU.mult),
                                 reads=[B_bba, B_sa], writes=[B_sa])
                            P.op("dve", lambda e: e.tensor_tensor(out=sbb[:, 0:nt], in0=bbb[:, 0:nt], in1=sbb[:, 0:nt], op=ALU.mult),
                                 reads=[B_bbb, B_sbb], writes=[B_sbb])
                            P.op("dve", lambda e: e.tensor_tensor(out=mg[:, c, 0:nt], in0=sa[:, 0:nt], in1=sbb[:, 0:nt], op=ALU.add),
                                 reads=[B_sa, B_sbb], writes=[B_mg])
                        pipe.push(post_br)
                pipe.flush()
                hhb = {}
                for cp in range(8):
                    if nxt is not None:
                        nn = nxt[2] // 128
                        if cp == 0:
                            head_o(nxt)
                        if 0 <= cp - 2 < nn:
                            head_b(nxt, cp - 2, hhb[cp - 2])
                        if cp < nn:
                            hhb[cp] = head_a(nxt, cp)
                    wo, B_wo = R["w"].next()
                    P.dma("pool", wo[:], w_out_v[:, :, cp * 256:(cp + 1) * 256], writes=[B_wo])
                    for half in range(2):
                        c = 2 * cp + half
                        cs = slice(half * 128, (half + 1) * 128)
                        bk, B_bk = R["pf"].next()
                        mm_group(bk[:, 0:nt], [(wo[:, kc, cs], mg[:, kc, 0:nt]) for kc in range(NKC)], [B_wo, B_mg], [B_bk])
                        pipe.push(lambda c=c, bk=bk, B_bk=B_bk, nt=nt: to_token_major(R, bk, B_bk, Rt, B_Rt, c, nt))
                pipe.flush()
            last = blocks3[-1]
            for s_ in range(last[2] // 128):
                epi_b(last, s_, epi_a(last, s_))
            P.barrier()
            P.emit()

        if STOP_AFTER[0] < 4:
            return nc
        with contextlib.ExitStack() as ps:
            R = {}
            R["junk"] = Ring(ps, nc, "junk", [128, D], BF16, 1)
            R["small"] = Ring(ps, nc, "small", [128, 1], F32, 8)
            R["w"] = Ring(ps, nc, "wsl", [128, NKC, 512], BF16, 2)
            R["wd"] = Ring(ps, nc, "wdn", [128, NFC, 128], BF16, 2)
            R["ub"] = Ring(ps, nc, "ub", [128, 514], F32, 4)
            R["tf"] = Ring(ps, nc, "tf", [128, 512], F32, 6)
            R["mst"] = Ring(ps, nc, "mst", [128, 512], F32, 2)
            R["x1"] = Ring(ps, nc, "x1s", [128, D], F32, 1)
            R["pf"] = Ring(ps, nc, "pf", [128, 512], F32, 8, psum=True)
            h2T = sb(ps, "h2T", [128, NKC, 514], BF16); B_h2T = Buf()
            gT = sb(ps, "gT", [128, NFC, 512], BF16); B_gT = Buf()
            Rt = sb(ps, "Rt4", [128, 4, D]); B_Rt = Buf()
            g3 = sb(ps, "g3", [128, D]); B_g3 = Buf()
            cw = sb(ps, "cw", [128, 3, 88]); B_cw = Buf()
            P.dma("sp", g3[:], gbc_d[3, :, :], writes=[B_g3])

            blocks4 = []
            for j in J:
                t = 0
                while t < j["NOUT"]:
                    n = min(512, j["NOUT"] - t)
                    blocks4.append((j, t, n))
                    t += n

            def load_h2T(blk):
                j, t0, nt = blk
                NQ, S = j["NQ"], j["S"]
                W_ = nt + 2
                lo, hi = t0 - 1, t0 + nt + 1
                c_lo, c_hi = 0, W_
                if lo < 0:
                    P.op("dve", lambda e: e.memset(h2T[:, :, 0:1], 0.0), writes=[B_h2T])
                    lo, c_lo = 0, 1
                if hi > NQ:
                    assert hi - 1 == S, "right halo missing"
                    P.op("dve", lambda e: e.memset(h2T[:, :, W_ - 1:W_], 0.0), writes=[B_h2T])
                    hi, c_hi = hi - 1, W_ - 1
                P.dma("sp", h2T[:, :, c_lo:c_hi], j["H2T"][:, :, lo:hi].rearrange("k p n -> p k n"), writes=[B_h2T])

            def epi4(blk, s_, nsz):
                j, t0, nt = blk
                x1s, B_x1s = R["x1"].next()
                r0 = t0 + s_ * 128
                P.dma_multi("sp", [(x1s[0:nsz, c_:c_ + 512], j["X1"][r0:r0 + nsz, c_:c_ + 512]) for c_ in range(0, D, 512)], writes=[B_x1s])
                resid_norm(R, Rt, B_Rt, s_, x1s[0:nsz, :], B_x1s, g3[0:nsz, :], B_g3, nsz)
                P.dma_multi("sp", [(j["y"][r0:r0 + nsz, c_:c_ + 512], Rt[0:nsz, s_, c_:c_ + 512]) for c_ in range(0, D, 512)], reads=[B_Rt])

            load_h2T(blocks4[0])
            cur_job = None
            for bi, blk in enumerate(blocks4):
                j, t0, nt = blk
                if j is not cur_job:
                    P.dma("sp", cw[:], j["convw"][:, :, :], writes=[B_cw])
                    cur_job = j
                prev = blocks4[bi - 1] if bi > 0 else None
                nxt = blocks4[bi + 1] if bi + 1 < len(blocks4) else None
                W_ = nt + 2
                esched = {}
                if prev is not None:
                    for k_, (s_, nsz) in enumerate(subtiles(prev[2])):
                        esched[1 + 3 * k_] = (s_, nsz)
                pipe = Pipe()
                for i in range(NFC):
                    if i in esched:
                        epi4(prev, *esched[i])
                    if i % 2 == 0:
                        wu, B_wu = R["w"].next()
                        P.dma_multi("pool", [(wu[:, :, 0:256], w_up_v[:, :, i * 128:i * 128 + 256]),
                                             (wu[:, :, 256:512], w_up_v[:, :, DFF + i * 128:DFF + i * 128 + 256])], writes=[B_wu])
                    bks = []
                    for half in range(2):
                        c0_ = half * 256 + (i % 2) * 128
                        cs = slice(c0_, c0_ + 128)
                        Wm = min(W_, 512)
                        bm, B_bm = R["pf"].next()
                        mm_group(bm[:, 0:Wm], [(wu[:, kc, cs], h2T[:, kc, 0:Wm]) for kc in range(NKC)], [B_wu, B_h2T], [B_bm])
                        bt, B_bt = None, None
                        if W_ > 512:
                            bt, B_bt = R["pf"].next()
                            mm_group(bt[:, 0:W_ - 512], [(wu[:, kc, cs], h2T[:, kc, 512:W_]) for kc in range(NKC)], [B_wu, B_h2T], [B_bt])
                        bks.append((bm, B_bm, bt, B_bt))

                    def post(i=i, bks=bks, nt=nt, W_=W_):
                        cv = []
                        for half in range(2):
                            bm, B_bm, bt, B_bt = bks[half]
                            ci = i + half * NFC
                            u, B_u = R["ub"].next()
                            Wm = min(W_, 512)
                            P.op("act", lambda e: e.activation(out=u[:, 0:Wm], in_=bm[:, 0:Wm], func=AF.Copy), reads=[B_bm], writes=[B_u])
                            if bt is not None:
                                P.op("act", lambda e: e.activation(out=u[:, 512:W_], in_=bt[:, 0:W_ - 512], func=AF.Copy), reads=[B_bt], writes=[B_u])
                            a_, B_a = R["tf"].next()
                            P.op("dve", lambda e: e.tensor_scalar(out=a_[:, 0:nt], in0=u[:, 0:nt], scalar1=cw[:, 0, ci:ci + 1], scalar2=convb[:, ci:ci + 1],
                                                                  op0=ALU.mult, op1=ALU.add),
                                 reads=[B_u, B_cw, B_convb], writes=[B_a])
                            P.op("dve", lambda e: e.scalar_tensor_tensor(out=a_[:, 0:nt], in0=u[:, 1:nt + 1], scalar=cw[:, 1, ci:ci + 1], in1=a_[:, 0:nt],
                                                                         op0=ALU.mult, op1=ALU.add),
                                 reads=[B_u, B_cw, B_a], writes=[B_a])
                            P.op("dve", lambda e: e.scalar_tensor_tensor(out=a_[:, 0:nt], in0=u[:, 2:nt + 2], scalar=cw[:, 2, ci:ci + 1], in1=a_[:, 0:nt],
                                                                         op0=ALU.mult, op1=ALU.add),
                                 reads=[B_u, B_cw, B_a], writes=[B_a])
                            cv.append((a_, B_a))
                        (a_, B_a), (b_, B_b) = cv
                        P.op("act", lambda e: e.activation(out=a_[:, 0:nt], in_=a_[:, 0:nt], func=AF.Gelu_apprx_tanh), reads=[B_a], writes=[B_a])
                        P.op("dve", lambda e: e.tensor_tensor(out=gT[:, i, 0:nt], in0=a_[:, 0:nt], in1=b_[:, 0:nt], op=ALU.mult),
                             reads=[B_a, B_b], writes=[B_gT])
                    pipe.push(post)
                pipe.flush()
                for c in range(NKC):
                    if c == 0 and nxt is not None:
                        load_h2T(nxt)
                    wd, B_wd = R["wd"].next()
                    P.dma("pool", wd[:], w_down_v[:, :, c * 128:(c + 1) * 128], writes=[B_wd])
                    bk, B_bk = R["pf"].next()
                    mm_group(bk[:, 0:nt], [(wd[:, kc, :], gT[:, kc, 0:nt]) for kc in range(NFC)], [B_wd, B_gT], [B_bk])
                    pipe.push(lambda c=c, bk=bk, B_bk=B_bk, nt=nt: to_token_major(R, bk, B_bk, Rt, B_Rt, c, nt))
                pipe.flush()
            last = blocks4[-1]
            for (s_, nsz) in subtiles(last[2]):
                epi4(last, s_, nsz)
            P.barrier()
            P.emit()
        nc._prog_stats = (P.n_ops, P.n_wait, dict(P.cnt), max(P.dma_cnt))
    return nc


def _rope_tables(S, reverse):
    t = np.arange(S)
    row = (t // GRID_W).astype(np.float32)
    col = (t % GRID_W).astype(np.float32)
    tf = t.astype(np.float32)
    inv64 = (np.float32(THETA) ** (-(np.arange(0, 64, 2, dtype=np.float32)) / np.float32(64))).astype(np.float32)
    inv128 = (np.float32(THETA) ** (-(np.arange(0, 128, 2, dtype=np.float32)) / np.float32(128))).astype(np.float32)
    angA = np.zeros((128, S), np.float32)
    for d in range(128):
        pos = row if d < 64 else col
        angA[d] = pos * inv64[(d % 64) % 32]
    angB = np.zeros((128, S), np.float32)
    for d in range(128):
        angB[d] = tf * inv128[d % 64]
    tab = np.stack([np.cos(angA.astype(np.float64)), np.sin(angA.astype(np.float64)),
                    np.cos(angB.astype(np.float64)), np.sin(angB.astype(np.float64))]).astype(np.float32)
    if reverse:
        tab = tab[:, :, ::-1]
    return np.ascontiguousarray(tab)


def _consts():
    bf = ml_dtypes.bfloat16
    identb = np.eye(128, dtype=np.float32)
    onesb = np.ones((128, 128), np.float32)
    rotA = np.zeros((128, 128), np.float32)
    for base in (0, 64):
        for m in range(base, base + 32):
            rotA[m + 32, m] = -1.0
        for m in range(base + 32, base + 64):
            rotA[m - 32, m] = 1.0
    rotB = np.zeros((128, 128), np.float32)
    for m in range(64):
        rotB[m + 64, m] = -1.0
    for m in range(64, 128):
        rotB[m - 64, m] = 1.0
    kk = np.arange(128)[:, None]
    qq = np.arange(128)[None, :]
    mP = np.tile((qq <= kk).astype(np.float32), (1, 4))
    mN = np.tile((kk <= qq).astype(np.float32), (1, 4))
    cbf = np.concatenate([identb, onesb, rotA, rotB, mP, mN], axis=1).astype(bf)
    return np.eye(128, dtype=np.float32), np.ascontiguousarray(cbf)


def _convw_layout(cw, reverse):
    if reverse:
        cw = cw[::-1]
    return np.ascontiguousarray(cw.reshape(3, 88, 128).transpose(2, 0, 1))


_CACHE = {}


def _get_program(jobs_key):
    if jobs_key not in _CACHE:
        jobs = [dict(name=n, S=S, NQ=NQ, NOUT=NOUT) for (n, S, NQ, NOUT) in jobs_key]
        _CACHE[jobs_key] = build_program(jobs)
    return _CACHE[jobs_key]


def run_cores(core_inputs, jobs_key):
    raise NotImplementedError


def kernel(x_prompt, x_sample, norm_pre_mix, w_in, q_norm_a, k_norm_a, sink_b, w_branch_a,
           w_branch_b, w_out, norm_post_mix, norm_pre_ffn, w_up, conv_w, conv_b, w_down,
           norm_post_ffn, _n_cores=8):
    f = lambda a: np.ascontiguousarray(np.asarray(a, dtype=np.float32))
    x_prompt = f(x_prompt); x_sample = f(x_sample)
    Bp, Sp, _ = x_prompt.shape
    Bs, Ss, _ = x_sample.shape
    half = Sp // 2
    jobs_key = (("p", Sp, half + 128, half), ("s", Ss, Ss, Ss))
    nc = _get_program(jobs_key)

    identf, cbf = _consts()
    gbc = np.ascontiguousarray(np.stack([np.broadcast_to(f(g)[0][None, :], (128, D))
                                         for g in (norm_pre_mix, norm_post_mix, norm_pre_ffn, norm_post_ffn)]))
    qkn = np.ascontiguousarray(np.stack([f(q_norm_a)[0], f(k_norm_a)[0]], axis=1))
    sinkbc = np.ascontiguousarray(np.broadcast_to(f(sink_b)[0][None, :], (128, 8)))
    convb = np.ascontiguousarray(f(conv_b)[0].reshape(88, 128).T)
    cw = f(conv_w)[0]
    shared = dict(w_in=f(w_in)[0], w_ba=f(w_branch_a)[0], w_bb=f(w_branch_b)[0], w_out=f(w_out)[0],
                  w_up=f(w_up)[0], w_down=f(w_down)[0], gbc=gbc, qkn=qkn, sinkbc=sinkbc, convb=convb,
                  identf=identf, cbf=cbf)
    tab_p = {False: _rope_tables(Sp, False), True: _rope_tables(Sp, True)}
    tab_s = _rope_tables(Ss, False)
    cw_l = {False: _convw_layout(cw, False), True: _convw_layout(cw, True)}

    n_cores = _n_cores
    in_maps = []
    for c in range(n_cores):
        p, hf = c // 2, c % 2
        rev = hf == 1
        xp = x_prompt[p % Bp]
        if rev:
            xp = np.ascontiguousarray(xp[::-1])
        m = dict(shared)
        m["x_p"] = xp
        m["tab_p"] = tab_p[rev]
        m["convw_p"] = cw_l[rev]
        m["x_s"] = x_sample[c % Bs]
        m["tab_s"] = tab_s
        m["convw_s"] = cw_l[False]
        in_maps.append(m)
    res = run_bass_kernel_spmd(nc, in_maps, core_ids=list(range(n_cores)))
    y_prompt = np.zeros((Bp, Sp, D), np.float32)
    y_sample = np.zeros((Bs, Ss, D), np.float32)
    for c in range(n_cores):
        p, hf = c // 2, c % 2
        r = res.results[c]
        yp = np.asarray(r["y_p"], dtype=np.float32)
        if hf == 0:
            y_prompt[p % Bp, 0:half] = yp
        else:
            y_prompt[p % Bp, half:] = yp[::-1]
        y_sample[c % Bs] = np.asarray(r["y_s"], dtype=np.float32)
    return (y_prompt, y_sample)
```

```python
import contextlib
import numpy as np
import ml_dtypes
import concourse.bass as bass
import concourse.mybir as mybir
from concourse.bass_utils import run_bass_kernel_spmd

F32 = mybir.dt.float32
BF16 = mybir.dt.bfloat16
AF = mybir.ActivationFunctionType
ALU = mybir.AluOpType

D = 2048
NKC = 16
DFF = 5632
NFC = 44
HD = 128
EPS = 1e-6
THETA = 10000.0
GRID_W = 64
SCALE = HD ** -0.5
DSPLIT = 352


class Buf:
    __slots__ = ("name", "w", "r", "excl")

    def __init__(self, name="", excl=False):
        self.name = name
        self.w = {}
        self.r = {}
        self.excl = excl


class _Rec:
    def __init__(self):
        self.call = None

    def __getattr__(self, name):
        def f(*a, **k):
            self.call = (name, a, k)
            return self
        return f


class Prog:
    ENGS = ("pe", "act", "dve", "pool", "sp")

    def __init__(self, nc, stack, n_dma_sems=32):
        self.nc = nc
        self.ops = {e: [] for e in self.ENGS}
        self.cnt = {e: 0 for e in self.ENGS}
        self.seen = {e: {} for e in self.ENGS}
        self.n_dma_sems = n_dma_sems
        self.dma_cnt = [0] * n_dma_sems
        self.n_sw = 8
        self.dma_rr = {"sw": 0, "hw": 0}
        self.esem = {e: stack.enter_context(nc.semaphore("s_" + e)) for e in self.ENGS}
        self.dsem = [stack.enter_context(nc.semaphore("d%d" % i)) for i in range(n_dma_sems)]
        self.n_ops = 0
        self.n_wait = 0

    def _needs(self, reads, writes):
        need = {}
        reads = [b for b in reads if b is not None]
        writes = [b for b in writes if b is not None]
        for b in reads:
            for k, v in b.w.items():
                if v > need.get(k, 0):
                    need[k] = v
        for b in writes:
            for k, v in b.w.items():
                if v > need.get(k, 0):
                    need[k] = v
            for k, v in b.r.items():
                if v > need.get(k, 0):
                    need[k] = v
        return need

    def _emit_waits(self, eng, need, skip_self=False):
        seen = self.seen[eng]
        for k, v in need.items():
            if skip_self and k == eng:
                continue
            if seen.get(k, 0) >= v:
                continue
            if isinstance(k, str):
                assert v <= self.cnt[k], ("wait on not-yet-signalled op", eng, k, v, self.cnt[k])
            seen[k] = v
            self.ops[eng].append(("wait", k, v))
            self.n_wait += 1

    def _mark(self, key, val, reads, writes, more=()):
        kvs = [(key, val)] + list(more)
        reads = [b for b in reads if b is not None]
        writes = [b for b in writes if b is not None]
        for b in reads:
            for k, v in kvs:
                if b.r.get(k, 0) < v:
                    b.r[k] = v
        for b in writes:
            b.w = dict(kvs)
            b.r = {}

    def op(self, eng, fn, reads=(), writes=(), signal=True):
        skip_self = (eng == "pe")
        if any(b is not None and b.excl for b in reads):
            writes = list(writes) + [b for b in reads if b is not None and b.excl]
            reads = [b for b in reads if b is None or not b.excl]
        need = self._needs(reads, writes)
        self._emit_waits(eng, need, skip_self=skip_self)
        if signal:
            self.cnt[eng] += 1
            val = self.cnt[eng]
        else:
            val = self.cnt[eng] + 1
        rec = _Rec()
        fn(rec)
        self.ops[eng].append(("op", rec.call, signal))
        self._mark(eng, val, reads, writes)
        self.n_ops += 1

    def dma(self, q, out, in_, reads=(), writes=()):
        need = self._needs(reads, writes)
        if q == "pool":
            i = self.dma_rr["sw"]
            self.dma_rr["sw"] = (i + 1) % self.n_sw
        else:
            i = self.n_sw + self.dma_rr["hw"]
            self.dma_rr["hw"] = (self.dma_rr["hw"] + 1) % (self.n_dma_sems - self.n_sw)
        key = ("d", i)
        if self.dma_cnt[i] > 0:
            need[key] = max(need.get(key, 0), self.dma_cnt[i])
        self._emit_waits(q, need)
        self.dma_cnt[i] += 16
        self.ops[q].append(("dma", out, in_, i))
        self._mark(key, self.dma_cnt[i], reads, writes)
        self.n_ops += 1

    def dma_multi(self, q, pieces, reads=(), writes=()):
        need = self._needs(reads, writes)
        kvs = []
        for (out, in_) in pieces:
            if q == "pool":
                i = self.dma_rr["sw"]
                self.dma_rr["sw"] = (i + 1) % self.n_sw
            else:
                i = self.n_sw + self.dma_rr["hw"]
                self.dma_rr["hw"] = (self.dma_rr["hw"] + 1) % (self.n_dma_sems - self.n_sw)
            key = ("d", i)
            if self.dma_cnt[i] > 0:
                need[key] = max(need.get(key, 0), self.dma_cnt[i])
            self._emit_waits(q, need)
            need = {}
            self.dma_cnt[i] += 16
            self.ops[q].append(("dma", out, in_, i))
            kvs.append((key, self.dma_cnt[i]))
            self.n_ops += 1
        self._mark(kvs[0][0], kvs[0][1], reads, writes, more=kvs[1:])

    def barrier(self):
        for e in self.ENGS:
            need = {}
            for k in self.ENGS:
                if k != e and self.cnt[k] > 0:
                    need[k] = self.cnt[k]
            for i in range(self.n_dma_sems):
                if self.dma_cnt[i] > 0:
                    need[("d", i)] = self.dma_cnt[i]
            self._emit_waits(e, need)

    def emit(self):
        nc = self.nc

        def semof(k):
            return self.esem[k] if isinstance(k, str) else self.dsem[k[1]]

        with nc.Block() as block:
            def run(engname):
                ops = self.ops[engname]

                def body(eng):
                    for o in ops:
                        if o[0] == "wait":
                            eng.wait_ge(semof(o[1]), o[2])
                        elif o[0] == "op":
                            ins = getattr(eng, o[1][0])(*o[1][1], **o[1][2])
                            if o[2]:
                                ins.then_inc(self.esem[engname], 1)
                        else:
                            eng.dma_start(out=o[1], in_=o[2]).then_inc(self.dsem[o[3]], 16)
                return body

            block.tensor(run("pe"))
            block.scalar(run("act"))
            block.vector(run("dve"))
            block.gpsimd(run("pool"))
            block.sync(run("sp"))
        self.ops = {e: [] for e in self.ENGS}


_UID = [0]


def _uname(name):
    _UID[0] += 1
    return "%s_u%d" % (name, _UID[0])


class Ring:
    def __init__(self, stack, nc, name, shape, dt, n, psum=False):
        self.items = []
        for i in range(n):
            if psum:
                t = stack.enter_context(nc.psum_tensor(_uname(name), shape, dt))
            else:
                t = stack.enter_context(nc.sbuf_tensor(_uname(name), shape, dt))
            self.items.append((t, Buf("%s%d" % (name, i), excl=psum)))
        self.i = 0

    def next(self):
        it = self.items[self.i]
        self.i = (self.i + 1) % len(self.items)
        return it


STOP_AFTER = [99]
import os as _os
DBG = _os.environ.get("KDBG", "")


def build_program(jobs):
    nc = bass.Bass("TRN2", target_bir_lowering=False)

    def din(name, shape, dt=F32):
        return nc.dram_tensor(name, shape, dt, kind="ExternalInput").ap()

    def dscr(name, shape, dt):
        return nc.dram_tensor(name, shape, dt, kind="Internal").ap()

    w_in = din("w_in", [D, 7168])
    w_ba = din("w_ba", [1024, D])
    w_bb = din("w_bb", [1024, D])
    w_out = din("w_out", [D, D])
    w_up = din("w_up", [D, 2 * DFF])
    w_down = din("w_down", [DFF, D])
    gbc_d = din("gbc", [4, 128, D])
    qkn_d = din("qkn", [128, 2])
    sink_d = din("sinkbc", [128, 8])
    convb_d = din("convb", [128, 88])
    identf_d = din("identf", [128, 128])
    cbf_d = din("cbf", [128, 1536], BF16)

    wsrc = [(w_in, [D, 7168]), (w_ba, [1024, D]), (w_bb, [1024, D]), (w_out, [D, D]), (w_up, [D, 2 * DFF]), (w_down, [DFF, D])]
    wbf = [dscr("wbf%d" % i, shp, BF16) for i, (_, shp) in enumerate(wsrc)]
    w_in_v, w_ba_v, w_bb_v, w_out_v, w_up_v, w_down_v = [w.rearrange("(kc p) c -> p kc c", p=128) for w in wbf]

    J = []
    for jb in jobs:
        n = jb["name"]
        S, NQ, NOUT = jb["S"], jb["NQ"], jb["NOUT"]
        NQP = ((NQ + 511) // 512) * 512
        j = dict(jb)
        j["NQP"] = NQP
        j["x"] = din("x_" + n, [S, D])
        j["tab"] = din("tab_" + n, [4, 128, S])
        j["convw"] = din("convw_" + n, [128, 3, 88])
        j["y"] = nc.dram_tensor("y_" + n, [NOUT, D], F32, kind="ExternalOutput").ap()
        j["KAT"] = dscr("KAT_" + n, [2, 128, S], BF16)
        j["KBT"] = dscr("KBT_" + n, [2, 128, S], BF16)
        j["VA"] = dscr("VA_" + n, [S, 256], BF16)
        j["VB"] = dscr("VB_" + n, [S, 256], BF16)
        j["QAT"] = dscr("QAT_" + n, [8, 128, NQP], BF16)
        j["QBT"] = dscr("QBT_" + n, [8, 128, NQP], BF16)
        j["OAT"] = dscr("OAT_" + n, [8, 128, NQP], BF16)
        j["OBT"] = dscr("OBT_" + n, [8, 128, NQP], BF16)
        j["X1"] = dscr("X1_" + n, [NQP, D], F32)
        j["H2T"] = dscr("H2T_" + n, [16, 128, NQP], BF16)
        j["B"] = {k: None for k in ("KAT", "KBT", "VA", "VB", "QAT", "QBT", "OAT", "OBT", "X1", "H2T")}
        J.append(j)
    SMAX = max(j["S"] for j in J)

    top = contextlib.ExitStack()
    with top:
        P = Prog(nc, top)

        def sb(stack, name, shape, dt=F32):
            return stack.enter_context(nc.sbuf_tensor(_uname(name), shape, dt))

        identf = sb(top, "identf", [128, 128]); B_identf = Buf()
        cbf = sb(top, "cbf", [128, 1536], BF16); B_cbf = Buf()
        onesf = sb(top, "onesf", [128, 128]); B_onesf = Buf()
        qkn = sb(top, "qkn", [128, 2]); B_qkn = Buf()
        epsb = sb(top, "epsb", [128, 1]); B_epsb = Buf()
        expsink = sb(top, "expsink", [128, 8]); B_expsink = Buf()
        convb = sb(top, "convb", [128, 88]); B_convb = Buf()
        identb = cbf[:, 0:128]
        onesb = cbf[:, 128:256]
        rotA = cbf[:, 256:384]
        rotB = cbf[:, 384:512]
        maskP = cbf[:, 512:1024]
        maskN = cbf[:, 1024:1536]
        P.dma("sp", identf[:], identf_d[:, :], writes=[B_identf])
        P.dma("sp", cbf[:], cbf_d[:, :], writes=[B_cbf])
        P.dma("sp", qkn[:], qkn_d[:, :], writes=[B_qkn])
        P.dma("sp", expsink[:], sink_d[:, :], writes=[B_expsink])
        P.dma("sp", convb[:], convb_d[:, :], writes=[B_convb])
        P.op("dve", lambda e: e.memset(onesf[:], 1.0), writes=[B_onesf])
        P.op("dve", lambda e: e.memset(epsb[:], EPS), writes=[B_epsb])
        P.op("act", lambda e: e.activation(out=expsink[:], in_=expsink[:], func=AF.Exp),
             reads=[B_expsink], writes=[B_expsink])

        late_casts = []
        for wi, ((wf, (K_, C_)), wb_) in enumerate(zip(wsrc, wbf)):
            for r0 in range(0, K_, 128):
                for c0 in range(0, C_, 2048):
                    cw_ = min(2048, C_ - c0)
                    if wi == 0:
                        P.dma("pool", wb_[r0:r0 + 128, c0:c0 + cw_], wf[r0:r0 + 128, c0:c0 + cw_])
                    else:
                        late_casts.append((wb_[r0:r0 + 128, c0:c0 + cw_], wf[r0:r0 + 128, c0:c0 + cw_]))
        P.barrier()

        def issue_late_casts(n):
            for _ in range(n):
                if late_casts:
                    d_, s_ = late_casts.pop(0)
                    P.dma("pool", d_, s_)
        if DBG == "T1":
            P.emit()
            return nc

        def mm(out, lhsT, rhs, start, stop, reads, writes, signal):
            P.op("pe", lambda e: e.matmul(out, lhsT=lhsT, rhs=rhs, start=start, stop=stop),
                 reads=reads, writes=writes, signal=signal)

        def mm_group(out, pairs, reads, writes):
            n = len(pairs)
            for i, (l, r) in enumerate(pairs):
                mm(out, l, r, i == 0, i == n - 1, reads, writes, i == n - 1)

        def rstd_from(ss_ap, B_ss, n, nsz=128):
            P.op("act", lambda e: e.activation(out=ss_ap, in_=ss_ap, func=AF.Sqrt, scale=1.0 / n, bias=epsb[0:nsz, 0:1]),
                 reads=[B_ss, B_epsb], writes=[B_ss])
            P.op("dve", lambda e: e.reciprocal(out=ss_ap, in_=ss_ap), reads=[B_ss], writes=[B_ss])

        class Pipe:
            def __init__(self, depth=1):
                self.q = []
                self.depth = depth

            def push(self, fn):
                self.q.append(fn)
                while len(self.q) > self.depth:
                    self.q.pop(0)()

            def flush(self):
                while self.q:
                    self.q.pop(0)()

        def norm_part(R, xs, B_xs, g_ap, B_g, nsz=128):
            ss, B_ss = R["small"].next()
            jk, B_jk = R["junk"].next()
            P.op("act", lambda e: e.activation(out=jk[0:nsz, :], in_=xs, func=AF.Square, accum_out=ss[0:nsz, 0:1]),
                 reads=[B_xs], writes=[B_jk, B_ss])
            rstd_from(ss[0:nsz, 0:1], B_ss, D, nsz)
            hb, B_hb = R["hb"].next()
            P.op("dve", lambda e: e.scalar_tensor_tensor(out=hb[0:nsz, :], in0=xs, scalar=ss[0:nsz, 0:1], in1=g_ap,
                                                          op0=ALU.mult, op1=ALU.mult),
                 reads=[B_xs, B_ss, B_g], writes=[B_hb])
            return hb, B_hb

        def transpose_part(R, hb, B_hb, hT, B_hT, col0, nsz=128):
            for q in range(4):
                tb, B_tb = R["pb"].next()
                for i in range(4):
                    kc = 4 * q + i
                    P.op("pe", lambda e, i=i, kc=kc, tb=tb: e.transpose(tb[:, i, 0:nsz], hb[0:nsz, kc * 128:(kc + 1) * 128], identb[0:nsz, 0:nsz]),
                         reads=[B_hb, B_cbf], writes=[B_tb], signal=(i == 3))
                P.op("act", lambda e, q=q, tb=tb: e.activation(out=hT[:, 4 * q:4 * q + 4, col0:col0 + nsz], in_=tb[:, 0:4, 0:nsz], func=AF.Copy),
                     reads=[B_tb], writes=[B_hT])

        def norm_transpose(R, xs, B_xs, g_ap, B_g, hT, B_hT, col0):
            hb, B_hb = norm_part(R, xs, B_xs, g_ap, B_g)
            transpose_part(R, hb, B_hb, hT, B_hT, col0)

        def load_x_block(R, xap, t0, nsub):
            hs = []
            for s in range(nsub):
                xs, B_xs = R["xst"].next()
                r0 = t0 + s * 128
                P.dma_multi("sp", [(xs[:, c_:c_ + 512], xap[r0:r0 + 128, c_:c_ + 512]) for c_ in range(0, D, 512)], writes=[B_xs])
                hs.append((xs, B_xs))
            return hs

        def subtiles(nt):
            return [(s, min(128, nt - s * 128)) for s in range((nt + 127) // 128)]

        def to_token_major(R, src_bank, B_src, Rt, B_Rt, c, nt):
            mst, B_mst = R["mst"].next()
            P.op("act", lambda e: e.activation(out=mst[:, 0:nt], in_=src_bank[:, 0:nt], func=AF.Copy),
                 reads=[B_src], writes=[B_mst])
            tb, B_tb = R["pf"].next()
            tbv = tb[:].rearrange("p (a b) -> p a b", a=4)
            st = subtiles(nt)
            for (s, nsz) in st:
                P.op("pe", lambda e, s=s, nsz=nsz: e.transpose(tbv[0:nsz, s, :], mst[:, s * 128:s * 128 + nsz], identf[:]),
                     reads=[B_mst, B_identf], writes=[B_tb], signal=(s == st[-1][0]))
            nfull = nt // 128
            if nfull:
                P.op("dve", lambda e: e.tensor_copy(out=Rt[:, 0:nfull, c * 128:(c + 1) * 128], in_=tbv[:, 0:nfull, :]),
                     reads=[B_tb], writes=[B_Rt])
            if nt % 128:
                nsz = nt % 128
                P.op("dve", lambda e: e.tensor_copy(out=Rt[0:nsz, nfull, c * 128:(c + 1) * 128], in_=tbv[0:nsz, nfull, :]),
                     reads=[B_tb], writes=[B_Rt])

        def resid_norm(R, Rt, B_Rt, s, xres, B_xres, g_ap, B_g, nsz=128):
            ss, B_ss = R["small"].next()
            jk, B_jk = R["junk"].next()
            P.op("act", lambda e: e.activation(out=jk[0:nsz, :], in_=Rt[0:nsz, s, :], func=AF.Square, accum_out=ss[0:nsz, 0:1]),
                 reads=[B_Rt], writes=[B_jk, B_ss])
            rstd_from(ss[0:nsz, 0:1], B_ss, D, nsz)
            P.op("dve", lambda e: e.scalar_tensor_tensor(out=Rt[0:nsz, s, :], in0=Rt[0:nsz, s, :], scalar=ss[0:nsz, 0:1], in1=g_ap,
                                                          op0=ALU.mult, op1=ALU.mult),
                 reads=[B_Rt, B_ss, B_g], writes=[B_Rt])
            P.op("dve", lambda e: e.tensor_tensor(out=Rt[0:nsz, s, :], in0=Rt[0:nsz, s, :], in1=xres, op=ALU.add),
                 reads=[B_Rt, B_xres], writes=[B_Rt])

        with contextlib.ExitStack() as ps:
            R = {}
            R["xst"] = Ring(ps, nc, "xst", [128, D], F32, 4)
            R["junk"] = Ring(ps, nc, "junk", [128, D], BF16, 1)
            R["hb"] = Ring(ps, nc, "hb", [128, D], BF16, 4)
            R["small"] = Ring(ps, nc, "small", [128, 1], F32, 8)
            R["hT"] = Ring(ps, nc, "hT", [128, NKC, 512], BF16, 2)
            R["w"] = Ring(ps, nc, "wsl", [128, NKC, 256], BF16, 4)
            R["tab"] = Ring(ps, nc, "tab", [128, 4, 512], F32, 2)
            R["tf"] = Ring(ps, nc, "tf", [128, 512], F32, 6)
            R["tb"] = Ring(ps, nc, "tb", [128, 512], BF16, 4)
            R["ob"] = Ring(ps, nc, "ob", [128, 512], BF16, 4)
            R["vst"] = Ring(ps, nc, "vst", [128, 256], BF16, 4)
            R["pf"] = Ring(ps, nc, "pf", [128, 512], F32, 6, psum=True)
            R["pb"] = Ring(ps, nc, "pb", [128, 8, 128], BF16, 2, psum=True)
            gbc0 = sb(ps, "gbc0", [128, D]); B_g0 = Buf()
            P.dma("sp", gbc0[:], gbc_d[0, :, :], writes=[B_g0])

            for j in J:
                S, NQ = j["S"], j["NQ"]
                JB = j["B"]
                blocks = [(t0, 512) for t0 in range(0, S, 512)]
                xl = load_x_block(R, j["x"], 0, 4)
                hbs = [norm_part(R, xl[s][0][:], xl[s][1], gbc0[:], B_g0) for s in range(4)]
                nxt_hT = R["hT"].next()
                for s in range(4):
                    transpose_part(R, hbs[s][0], hbs[s][1], nxt_hT[0], nxt_hT[1], s * 128)
                for bi, (t0, nt) in enumerate(blocks):
                    own = t0 < NQ
                    hT, B_hT = nxt_hT
                    has_next = bi + 1 < len(blocks)
                    tab, B_tab = R["tab"].next()
                    P.dma("sp", tab[:], j["tab"][:, :, t0:t0 + 512].rearrange("f p n -> p f n"), writes=[B_tab])

                    slabs = []
                    if own:
                        slabs += [("qA", c0) for c0 in (0, 256, 512, 768)]
                    slabs += [("kA", 1024), ("vA", 1280)]
                    if own:
                        slabs += [("qB", c0) for c0 in (1536, 1792, 2048, 2304)]
                    slabs += [("kB", 2560), ("vB", 2816)]
                    pipe = Pipe()
                    vpos = [i_ for i_, (k_, _) in enumerate(slabs) if k_ == "vA"][0]
                    if vpos >= 4:
                        ldsched = {vpos - 4: [0], vpos - 3: [1], vpos - 2: [2], vpos - 1: [3]}
                    else:
                        ldsched = {0: [0, 1], vpos: [2, 3]}
                    if has_next:
                        hbs = [None] * 4
                        xls = [None] * 4
                    for si, (kind, c0) in enumerate(slabs):
                        if has_next and si in ldsched:
                            for s_ in ldsched[si]:
                                (xls[s_],) = load_x_block(R, j["x"], blocks[bi + 1][0] + s_ * 128, 1)
                        if has_next and si == vpos:
                            for s_ in range(4):
                                hbs[s_] = norm_part(R, xls[s_][0][:], xls[s_][1], gbc0[:], B_g0)
                        if has_next and si == len(slabs) - 2:
                            nxt_hT = R["hT"].next()
                            for s in range(4):
                                transpose_part(R, hbs[s][0], hbs[s][1], nxt_hT[0], nxt_hT[1], s * 128)
                        wt, B_wt = R["w"].next()
                        P.dma("pool", wt[:], w_in_v[:, :, c0:c0 + 256], writes=[B_wt])
                        if kind[0] == "v":
                            vd, B_vd = (j["VA"], JB["VA"]) if kind == "vA" else (j["VB"], JB["VB"])
                            for s in range(4):
                                bk, B_bk = R["pf"].next()
                                mm_group(bk[:, 0:256], [(hT[:, kc, s * 128:(s + 1) * 128], wt[:, kc, :]) for kc in range(NKC)],
                                         [B_hT, B_wt], [B_bk])

                                def vpost(bk=bk, B_bk=B_bk, s=s, vd=vd, B_vd=B_vd):
                                    vs, B_vs = R["vst"].next()
                                    P.op("act", lambda e: e.activation(out=vs[:], in_=bk[:, 0:256], func=AF.Copy),
                                         reads=[B_bk], writes=[B_vs])
                                    r0 = t0 + s * 128
                                    P.dma("sp", vd[r0:r0 + 128, :], vs[:], reads=[B_vs], writes=[B_vd])
                                if DBG not in ("T3", "T5"):
                                    pipe.push(vpost)
                            continue
                        for half in range(2):
                            bk, B_bk = R["pf"].next()
                            mm_group(bk[:], [(wt[:, kc, half * 128:(half + 1) * 128], hT[:, kc, :]) for kc in range(NKC)],
                                     [B_hT, B_wt], [B_bk])
                            if kind == "qA":
                                idx = c0 // 128 + half
                                dst, B_dst = j["QAT"][idx, :, t0:t0 + 512], JB["QAT"]
                            elif kind == "qB":
                                idx = (c0 - 1536) // 128 + half
                                dst, B_dst = j["QBT"][idx, :, t0:t0 + 512], JB["QBT"]
                            elif kind == "kA":
                                dst, B_dst = j["KAT"][half, :, t0:t0 + 512], JB["KAT"]
                            else:
                                dst, B_dst = j["KBT"][half, :, t0:t0 + 512], JB["KBT"]
                            isA = kind[1] == "A"

                            def post(bk=bk, B_bk=B_bk, isA=isA, kind=kind, dst=dst, B_dst=B_dst, tab=tab, B_tab=B_tab):
                                qg, B_qg = R["tb"].next()
                                t1, B_t1 = R["tf"].next()
                                t2, B_t2 = R["tf"].next()
                                ob, B_ob = R["ob"].next()
                                b3, B_b3 = R["pf"].next()
                                if isA:
                                    sq, B_sq = R["tf"].next()
                                    col = 0 if kind[0] == "q" else 1
                                    P.op("act", lambda e: e.activation(out=sq[:], in_=bk[:], func=AF.Square),
                                         reads=[B_bk], writes=[B_sq])
                                    P.op("act", lambda e: e.activation(out=qg[:], in_=bk[:], func=AF.Copy, scale=qkn[:, col:col + 1]),
                                         reads=[B_bk, B_qkn], writes=[B_qg])
                                    b2, B_b2 = R["pf"].next()
                                    mm(b2[:], onesf[:], sq[:], True, True, [B_onesf, B_sq], [B_b2], True)
                                    mm(b3[:], rotA, qg[:], True, True, [B_cbf, B_qg], [B_b3], True)
                                    rs, B_rs = sq, B_sq
                                    P.op("act", lambda e: e.activation(out=rs[:], in_=b2[:], func=AF.Ln, scale=1.0 / HD, bias=epsb[:, 0:1]),
                                         reads=[B_b2, B_epsb], writes=[B_rs])
                                    P.op("act", lambda e: e.activation(out=rs[:], in_=rs[:], func=AF.Exp, scale=-0.5), reads=[B_rs], writes=[B_rs])
                                    P.op("dve", lambda e: e.tensor_tensor(out=t1[:], in0=qg[:], in1=tab[:, 0, :], op=ALU.mult),
                                         reads=[B_qg, B_tab], writes=[B_t1])
                                    P.op("dve", lambda e: e.tensor_tensor(out=t2[:], in0=b3[:], in1=tab[:, 1, :], op=ALU.mult),
                                         reads=[B_b3, B_tab], writes=[B_t2])
                                    P.op("dve", lambda e: e.tensor_tensor(out=t1[:], in0=t1[:], in1=t2[:], op=ALU.add),
                                         reads=[B_t1, B_t2], writes=[B_t1])
                                    P.op("dve", lambda e: e.tensor_tensor(out=ob[:], in0=t1[:], in1=rs[:], op=ALU.mult),
                                         reads=[B_t1, B_rs], writes=[B_ob])
                                else:
                                    P.op("act", lambda e: e.activation(out=qg[:], in_=bk[:], func=AF.Copy),
                                         reads=[B_bk], writes=[B_qg])
                                    mm(b3[:], rotB, qg[:], True, True, [B_cbf, B_qg], [B_b3], True)
                                    P.op("dve", lambda e: e.tensor_tensor(out=t1[:], in0=bk[:], in1=tab[:, 2, :], op=ALU.mult),
                                         reads=[B_bk, B_tab], writes=[B_t1])
                                    P.op("dve", lambda e: e.tensor_tensor(out=t2[:], in0=b3[:], in1=tab[:, 3, :], op=ALU.mult),
                                         reads=[B_b3, B_tab], writes=[B_t2])
                                    P.op("dve", lambda e: e.tensor_tensor(out=ob[:], in0=t1[:], in1=t2[:], op=ALU.add),
                                         reads=[B_t1, B_t2], writes=[B_ob])
                                P.dma("sp", dst, ob[:], reads=[B_ob], writes=[B_dst])
                            if DBG not in ("T3", "T4"):
                                pipe.push(post)
                    pipe.flush()
            P.barrier()
            P.emit()

        if STOP_AFTER[0] < 2:
            return nc
        with contextlib.ExitStack() as ps:
            R = {}
            NKB = SMAX // 128
            R["K"] = Ring(ps, nc, "Kt", [128, SMAX], BF16, 2)
            R["V"] = Ring(ps, nc, "Vt", [128, NKB, 128], BF16, 2)
            R["Q"] = Ring(ps, nc, "Qt", [128, 4, 512], BF16, 2)
            R["QB"] = Ring(ps, nc, "QBt", [128, 8, 512], BF16, 2)
            R["obst"] = Ring(ps, nc, "obst", [128, 8, 512], BF16, 1)
            R["pt"] = Ring(ps, nc, "pt", [128, 2, 512], BF16, 6)
            R["ptB"] = Ring(ps, nc, "ptB", [128, 512], BF16, 6)
            R["accD"] = Ring(ps, nc, "accD", [128, 2, DSPLIT], F32, 2)
            R["accP"] = Ring(ps, nc, "accP", [128, 2, 512 - DSPLIT], F32, 2)
            R["tf"] = Ring(ps, nc, "tf", [128, 512], F32, 3)
            R["ob"] = Ring(ps, nc, "ob", [128, 512], BF16, 2)
            R["pS2"] = Ring(ps, nc, "pS2", [128, 2, 512], F32, 2, psum=True)
            R["pO"] = Ring(ps, nc, "pO", [128, 512], F32, 2, psum=True)
            R["pM"] = Ring(ps, nc, "pM", [128, 512], F32, 2, psum=True)
            KB0 = sb(ps, "KB0", [128, SMAX], BF16); KB1 = sb(ps, "KB1", [128, SMAX], BF16)
            VBt = sb(ps, "VBt", [128, NKB, 256], BF16)
            B_KB = [Buf(), Buf()]; B_VBt = Buf()
            sinkrow = sb(ps, "sinkrow", [128, 2, 512]); B_sinkrow = Buf()
            zer = sb(ps, "zer", [128, 128]); B_zer = Buf()
            P.op("dve", lambda e: e.memset(zer[:], 0.0), writes=[B_zer])
            for h in range(8):
                P.op("act", lambda e, h=h: e.activation(out=sinkrow[:, h // 4, (h % 4) * 128:(h % 4 + 1) * 128], in_=zer[:],
                                                        func=AF.Identity, bias=expsink[:, h:h + 1]),
                     reads=[B_zer, B_expsink], writes=[B_sinkrow])
            KBs = [KB0, KB1]

            def make_bunit(j, t0, nt, sj, g, first, lastu, blk, nkb):
                qbi = t0 // 128 + sj
                kbs = [k for k in (qbi - 1, qbi, qbi + 1) if 0 <= k < nkb]
                st = {}

                def s1():
                    if first:
                        blk["QB"] = R["QB"].next()
                        blk["obst"] = R["obst"].next()
                        QB, B_QB = blk["QB"]
                        P.dma("sp", QB[:, :, 0:nt], j["QBT"][:, :, t0:t0 + nt].rearrange("h p n -> p h n"), writes=[B_QB])
                    QB, B_QB = blk["QB"]
                    pts = []
                    for k in kbs:
                        sbk2, B_sbk = R["pS2"].next()
                        sbk = sbk2[:, 0, :]
                        sbv = sbk.rearrange("p (a b) -> p a b", a=4)
                        mm(sbv, KBs[g][:, k * 128:(k + 1) * 128], QB[:, 4 * g:4 * g + 4, sj * 128:(sj + 1) * 128], True, True,
                           [B_KB[g], B_QB], [B_sbk], True)
                        pt, B_pt = R["ptB"].next()
                        P.op("act", lambda e, pt=pt, sbk=sbk: e.activation(out=pt[:], in_=sbk, func=AF.Exp, scale=SCALE),
                             reads=[B_sbk], writes=[B_pt])
                        if k != qbi:
                            msk = maskP if k < qbi else maskN
                            P.op("dve", lambda e, pt=pt, msk=msk: e.tensor_tensor(out=pt[:], in0=pt[:], in1=msk, op=ALU.mult),
                                 reads=[B_pt, B_cbf], writes=[B_pt])
                        pts.append((k, pt, B_pt))
                    st["pts"] = pts

                def s2():
                    pts = st["pts"]
                    obst, B_obst = blk["obst"]
                    O, B_O = R["pO"].next()
                    M, B_M = R["pM"].next()
                    n = len(pts)
                    for ii, (k, pt, B_pt) in enumerate(pts):
                        last = ii == n - 1
                        mm(O[:], VBt[:, k, g * 128:(g + 1) * 128], pt[:], ii == 0, last, [B_VBt, B_pt], [B_O], last)
                        mm(M[:], onesb, pt[:], ii == 0, last, [B_cbf, B_pt], [B_M], last)
                    rc, B_rc = R["tf"].next()
                    P.op("dve", lambda e: e.tensor_tensor(out=rc[:], in0=M[:], in1=sinkrow[:, g, :], op=ALU.add),
                         reads=[B_M, B_sinkrow], writes=[B_rc])
                    P.op("act", lambda e: e.activation(out=rc[:], in_=rc[:], func=AF.Ln), reads=[B_rc], writes=[B_rc])
                    P.op("act", lambda e: e.activation(out=rc[:], in_=rc[:], func=AF.Exp, scale=-1.0), reads=[B_rc], writes=[B_rc])
                    P.op("dve", lambda e: e.tensor_tensor(
                        out=obst[:, 4 * g:4 * g + 4, sj * 128:(sj + 1) * 128],
                        in0=O[:].rearrange("p (a b) -> p a b", a=4),
                        in1=rc[:].rearrange("p (a b) -> p a b", a=4), op=ALU.mult),
                         reads=[B_O, B_rc], writes=[B_obst])
                    if lastu:
                        P.dma("sp", j["OBT"][:, :, t0:t0 + nt].rearrange("h p n -> p h n"), obst[:, :, 0:nt], reads=[B_obst])
                return s1, s2

            for j in J:
                S, NQ = j["S"], j["NQ"]
                nkb = S // 128
                qblocks = [(t0, min(512, NQ - t0)) for t0 in range(0, NQ, 512)]
                for g in range(2):
                    P.dma("sp", KBs[g][:, 0:S], j["KBT"][g, :, :], writes=[B_KB[g]])
                P.dma("sp", VBt[:, 0:nkb, :], j["VB"].rearrange("(kb p) c -> p kb c", p=128), writes=[B_VBt])
                bunits = []
                for (t0, nt) in qblocks:
                    nsub = nt // 128
                    blk = {}
                    for sj in range(nsub):
                        for g in range(2):
                            bunits.append(make_bunit(j, t0, nt, sj, g, sj == 0 and g == 0, sj == nsub - 1 and g == 1, blk, nkb))
                ui = 0
                for g in range(2):
                    Kt, B_Kt = R["K"].next()
                    Vt, B_Vt = R["V"].next()
                    P.dma("sp", Kt[:, 0:S], j["KAT"][g, :, :], writes=[B_Kt])
                    P.dma("sp", Vt[:, 0:nkb, :], j["VA"][:, g * 128:(g + 1) * 128].rearrange("(kb p) d -> p kb d", p=128), writes=[B_Vt])
                    for (t0, nt) in qblocks:
                        Qt, B_Qt = R["Q"].next()
                        P.dma("sp", Qt[:, :, 0:nt], j["QAT"][4 * g:4 * g + 4, :, t0:t0 + nt].rearrange("h p n -> p h n"), writes=[B_Qt])
                        for hh in range(4):
                            bu = bunits[ui] if ui < len(bunits) else None
                            ui += 1
                            issue_late_casts(2)
                            if bu is not None:
                                bu[0]()
                            O, B_O = R["pO"].next()
                            accD, B_accD = R["accD"].next()
                            accP, B_accP = R["accP"].next()
                            dsp = min(DSPLIT, nt)
                            pipe = Pipe(3)
                            assert nkb % 2 == 0
                            for kp in range(nkb // 2):
                                sb2, B_sb2 = R["pS2"].next()
                                for u in range(2):
                                    kb = 2 * kp + u
                                    mm(sb2[:, u, 0:nt], Kt[:, kb * 128:(kb + 1) * 128], Qt[:, hh, 0:nt], True, True,
                                       [B_Kt, B_Qt], [B_sb2], u == 1)
                                pt, B_pt = R["pt"].next()
                                P.op("act", lambda e: e.activation(out=pt[:, :, 0:nt], in_=sb2[:, :, 0:nt], func=AF.Exp, scale=SCALE),
                                     reads=[B_sb2], writes=[B_pt])
                                if kp == 0:
                                    P.op("dve", lambda e: e.tensor_copy(out=accD[:, :, 0:dsp], in_=pt[:, :, 0:dsp]), reads=[B_pt], writes=[B_accD])
                                    if nt > dsp:
                                        P.op("pool", lambda e: e.tensor_copy(out=accP[:, :, 0:nt - dsp], in_=pt[:, :, dsp:nt]), reads=[B_pt], writes=[B_accP])
                                else:
                                    P.op("dve", lambda e: e.tensor_tensor(out=accD[:, :, 0:dsp], in0=accD[:, :, 0:dsp], in1=pt[:, :, 0:dsp], op=ALU.add),
                                         reads=[B_pt, B_accD], writes=[B_accD])
                                    if nt > dsp:
                                        P.op("pool", lambda e: e.tensor_tensor(out=accP[:, :, 0:nt - dsp], in0=accP[:, :, 0:nt - dsp], in1=pt[:, :, dsp:nt], op=ALU.add),
                                             reads=[B_pt, B_accP], writes=[B_accP])

                                def pv(kp=kp, pt=pt, B_pt=B_pt, O=O, B_O=B_O):
                                    for u in range(2):
                                        kb = 2 * kp + u
                                        last = kb == nkb - 1
                                        mm(O[:, 0:nt], Vt[:, kb, :], pt[:, u, 0:nt], kb == 0, last, [B_Vt, B_pt], [B_O], last)
                                pipe.push(pv)
                            pipe.flush()
                            M, B_M = R["pM"].next()
                            mm(M[:, 0:dsp], onesf[:], accD[:, 0, 0:dsp], True, False, [B_onesf, B_accD], [B_M], False)
                            mm(M[:, 0:dsp], onesf[:], accD[:, 1, 0:dsp], False, True, [B_onesf, B_accD], [B_M], True)
                            if nt > dsp:
                                mm(M[:, dsp:nt], onesf[:], accP[:, 0, 0:nt - dsp], True, False, [B_onesf, B_accP], [B_M], False)
                                mm(M[:, dsp:nt], onesf[:], accP[:, 1, 0:nt - dsp], False, True, [B_onesf, B_accP], [B_M], True)
                            rc, B_rc = R["tf"].next()
                            ob, B_ob = R["ob"].next()
                            P.op("act", lambda e, rc=rc, M=M: e.activation(out=rc[:, 0:nt], in_=M[:, 0:nt], func=AF.Ln), reads=[B_M], writes=[B_rc])
                            P.op("act", lambda e, rc=rc: e.activation(out=rc[:, 0:nt], in_=rc[:, 0:nt], func=AF.Exp, scale=-1.0), reads=[B_rc], writes=[B_rc])
                            P.op("dve", lambda e, rc=rc, O=O, ob=ob: e.tensor_tensor(out=ob[:, 0:nt], in0=O[:, 0:nt], in1=rc[:, 0:nt], op=ALU.mult),
                                 reads=[B_O, B_rc], writes=[B_ob])
                            P.dma("sp", j["OAT"][4 * g + hh, :, t0:t0 + nt], ob[:, 0:nt], reads=[B_ob])
                            if bu is not None:
                                bu[1]()
                while ui < len(bunits):
                    bunits[ui][0]()
                    bunits[ui][1]()
                    ui += 1
            issue_late_casts(len(late_casts))
            P.barrier()
            P.emit()

        if STOP_AFTER[0] < 3:
            return nc
        with contextlib.ExitStack() as ps:
            R = {}
            R["xst"] = Ring(ps, nc, "xst", [128, D], F32, 2)
            R["junk"] = Ring(ps, nc, "junk", [128, D], BF16, 1)
            R["hb"] = Ring(ps, nc, "hb", [128, D], BF16, 3)
            R["small"] = Ring(ps, nc, "small", [128, 1], F32, 8)
            R["w"] = Ring(ps, nc, "wsl", [128, NKC, 256], BF16, 3)
            R["wb"] = Ring(ps, nc, "wbr", [128, 8, 256], BF16, 3)
            R["tf"] = Ring(ps, nc, "tf", [128, 512], F32, 4)
            R["mst"] = Ring(ps, nc, "mst", [128, 512], F32, 2)
            R["pf"] = Ring(ps, nc, "pf", [128, 512], F32, 6, psum=True)
            R["pb"] = Ring(ps, nc, "pb", [128, 8, 128], BF16, 2, psum=True)
            hT = sb(ps, "hT3", [128, NKC, 512], BF16); B_hT = Buf()
            h2st = sb(ps, "h2st", [128, NKC, 512], BF16); B_h2st = Buf()
            oA = sb(ps, "oA", [128, 8, 512], BF16); B_oA = Buf()
            oB = sb(ps, "oB", [128, 8, 512], BF16); B_oB = Buf()
            mg = sb(ps, "mg", [128, NKC, 512], BF16); B_mg = Buf()
            Rt = sb(ps, "Rt", [128, 4, D]); B_Rt = Buf()
            g0 = sb(ps, "g0", [128, D]); g1 = sb(ps, "g1", [128, D]); g2 = sb(ps, "g2", [128, D])
            B_gs = [Buf(), Buf(), Buf()]
            for i, gt in enumerate((g0, g1, g2)):
                P.dma("sp", gt[:], gbc_d[i, :, :], writes=[B_gs[i]])

            blocks3 = [(j, t0, min(512, j["NQ"] - t0)) for j in J for t0 in range(0, j["NQ"], 512)]

            def head_a(blk, s_):
                j, t0, nt = blk
                (xs_, B_xs_), = load_x_block(R, j["x"], t0 + s_ * 128, 1)
                return norm_part(R, xs_[:], B_xs_, g0[:], B_gs[0])

            def head_b(blk, s_, hb_):
                transpose_part(R, hb_[0], hb_[1], hT, B_hT, s_ * 128)

            def head_o(blk):
                j, t0, nt = blk
                P.dma("sp", oA[:, :, 0:nt], j["OAT"][:, :, t0:t0 + nt].rearrange("h p n -> p h n"), writes=[B_oA])
                P.dma("sp", oB[:, :, 0:nt], j["OBT"][:, :, t0:t0 + nt].rearrange("h p n -> p h n"), writes=[B_oB])

            def epi_a(blk, s_):
                j, t0, nt = blk
                (xs_, B_xs_), = load_x_block(R, j["x"], t0 + s_ * 128, 1)
                resid_norm(R, Rt, B_Rt, s_, xs_[:], B_xs_, g1[:], B_gs[1])
                r0 = t0 + s_ * 128
                P.dma_multi("sp", [(j["X1"][r0:r0 + 128, c_:c_ + 512], Rt[:, s_, c_:c_ + 512]) for c_ in range(0, D, 512)], reads=[B_Rt])
                return norm_part(R, Rt[:, s_, :], B_Rt, g2[:], B_gs[2])

            def epi_b(blk, s_, hb_):
                j, t0, nt = blk
                transpose_part(R, hb_[0], hb_[1], h2st, B_h2st, s_ * 128)
                if s_ == nt // 128 - 1:
                    P.dma("sp", j["H2T"][:, :, t0:t0 + nt].rearrange("k p n -> p k n"), h2st[:, :, 0:nt], reads=[B_h2st])

            for s_ in range(blocks3[0][2] // 128):
                head_b(blocks3[0], s_, head_a(blocks3[0], s_))
            head_o(blocks3[0])

            for bi, blk in enumerate(blocks3):
                j, t0, nt = blk
                nsub = nt // 128
                prev = blocks3[bi - 1] if bi > 0 else None
                nxt = blocks3[bi + 1] if bi + 1 < len(blocks3) else None
                ehb = {}
                pipe = Pipe()
                for cp in range(8):
                    if prev is not None:
                        pn = prev[2] // 128
                        if 0 <= cp - 2 < pn:
                            epi_b(prev, cp - 2, ehb[cp - 2])
                        if cp < pn:
                            ehb[cp] = epi_a(prev, cp)
                    wga, B_wga = R["w"].next()
                    P.dma("pool", wga[:], w_in_v[:, :, 3072 + cp * 256:3072 + (cp + 1) * 256], writes=[B_wga])
                    wgb, B_wgb = R["w"].next()
                    P.dma("pool", wgb[:], w_in_v[:, :, 5120 + cp * 256:5120 + (cp + 1) * 256], writes=[B_wgb])
                    wba, B_wba = R["wb"].next()
                    P.dma("pool", wba[:], w_ba_v[:, :, cp * 256:(cp + 1) * 256], writes=[B_wba])
                    wbb, B_wbb = R["wb"].next()
                    P.dma("pool", wbb[:], w_bb_v[:, :, cp * 256:(cp + 1) * 256], writes=[B_wbb])
                    for half in range(2):
                        c = 2 * cp + half
                        cs = slice(half * 128, (half + 1) * 128)
                        bga, B_bga = R["pf"].next()
                        mm_group(bga[:, 0:nt], [(wga[:, kc, cs], hT[:, kc, 0:nt]) for kc in range(NKC)], [B_wga, B_hT], [B_bga])
                        bgb, B_bgb = R["pf"].next()
                        mm_group(bgb[:, 0:nt], [(wgb[:, kc, cs], hT[:, kc, 0:nt]) for kc in range(NKC)], [B_wgb, B_hT], [B_bgb])
                        sa, B_sa = R["tf"].next()
                        sbb, B_sbb = R["tf"].next()

                        def post_gate(bga=bga, B_bga=B_bga, bgb=bgb, B_bgb=B_bgb, sa=sa, B_sa=B_sa, sbb=sbb, B_sbb=B_sbb, nt=nt):
                            P.op("act", lambda e: e.activation(out=sa[:, 0:nt], in_=bga[:, 0:nt], func=AF.Sigmoid), reads=[B_bga], writes=[B_sa])
                            P.op("act", lambda e: e.activation(out=sbb[:, 0:nt], in_=bgb[:, 0:nt], func=AF.Sigmoid), reads=[B_bgb], writes=[B_sbb])
                        pipe.push(post_gate)
                        bba, B_bba = R["pf"].next()
                        mm_group(bba[:, 0:nt], [(wba[:, kc, cs], oA[:, kc, 0:nt]) for kc in range(8)], [B_wba, B_oA], [B_bba])
                        bbb, B_bbb = R["pf"].next()
                        mm_group(bbb[:, 0:nt], [(wbb[:, kc, cs], oB[:, kc, 0:nt]) for kc in range(8)], [B_wbb, B_oB], [B_bbb])

                        def post_br(c=c, bba=bba, B_bba=B_bba, bbb=bbb, B_bbb=B_bbb, sa=sa, B_sa=B_sa, sbb=sbb, B_sbb=B_sbb, nt=nt):
                            P.op("dve", lambda e: e.tensor_tensor(out=sa[:, 0:nt], in0=bba[:, 0:nt], in1=sa[:, 0:nt], op=ALU.mult),
                                 reads=[B_bba, B_sa], writes=[B_sa])
                            P.op("dve", lambda e: e.tensor_tensor(out=sbb[:, 0:nt], in0=bbb[:, 0:nt], in1=sbb[:, 0:nt], op=ALU.mult),
                                 reads=[B_bbb, B_sbb], writes=[B_sbb])
                            P.op("dve", lambda e: e.tensor_tensor(out=mg[:, c, 0:nt], in0=sa[:, 0:nt], in1=sbb[:, 0:nt], op=ALU.add),
                                 reads=[B_sa, B_sbb], writes=[B_mg])
                        pipe.push(post_br)
                pipe.flush()
                hhb = {}
                for cp in range(8):
                    if nxt is not None:
                        nn = nxt[2] // 128
                        if cp == 0:
                            head_o(nxt)
                        if 0 <= cp - 2 < nn:
                            head_b(nxt, cp - 2, hhb[cp - 2])
                        if cp < nn:
                            hhb[cp] = head_a(nxt, cp)
                    wo, B_wo = R["w"].next()
                    P.dma("pool", wo[:], w_out_v[:, :, cp * 256:(cp + 1) * 256], writes=[B_wo])
                    for half in range(2):
                        c = 2 * cp + half
                        cs = slice(half * 128, (half + 1) * 128)
                        bk, B_bk = R["pf"].next()
                        mm_group(bk[:, 0:nt], [(wo[:, kc, cs], mg[:, kc, 0:nt]) for kc in range(NKC)], [B_wo, B_mg], [B_bk])
                        pipe.push(lambda c=c, bk=bk, B_bk=B_bk, nt=nt: to_token_major(R, bk, B_bk, Rt, B_Rt, c, nt))
                pipe.flush()
            last = blocks3[-1]
            for s_ in range(last[2] // 128):
                epi_b(last, s_, epi_a(last, s_))
            P.barrier()
            P.emit()

        if STOP_AFTER[0] < 4:
            return nc
        with contextlib.ExitStack() as ps:
            R = {}
            R["junk"] = Ring(ps, nc, "junk", [128, D], BF16, 1)
            R["small"] = Ring(ps, nc, "small", [128, 1], F32, 8)
            R["w"] = Ring(ps, nc, "wsl", [128, NKC, 512], BF16, 2)
            R["wd"] = Ring(ps, nc, "wdn", [128, NFC // 2, 256], BF16, 2)
            R["ub"] = Ring(ps, nc, "ub", [128, 514], F32, 4)
            R["tf"] = Ring(ps, nc, "tf", [128, 512], F32, 6)
            R["mst"] = Ring(ps, nc, "mst", [128, 512], F32, 2)
            R["x1"] = Ring(ps, nc, "x1s", [128, D], F32, 1)
            R["pf"] = Ring(ps, nc, "pf", [128, 512], F32, 8, psum=True)
            h2T = sb(ps, "h2T", [128, NKC, 514], BF16); B_h2T = Buf()
            gT = sb(ps, "gT", [128, NFC, 512], BF16); B_gT = Buf()
            Rt = sb(ps, "Rt4", [128, 4, D]); B_Rt = Buf()
            g3 = sb(ps, "g3", [128, D]); B_g3 = Buf()
            cw = sb(ps, "cw", [128, 3, 88]); B_cw = Buf()
            P.dma("sp", g3[:], gbc_d[3, :, :], writes=[B_g3])

            blocks4 = []
            for j in J:
                t = 0
                while t < j["NOUT"]:
                    n = min(512, j["NOUT"] - t)
                    blocks4.append((j, t, n))
                    t += n

            def load_h2T(blk):
                j, t0, nt = blk
                NQ, S = j["NQ"], j["S"]
                W_ = nt + 2
                lo, hi = t0 - 1, t0 + nt + 1
                c_lo, c_hi = 0, W_
                if lo < 0:
                    P.op("dve", lambda e: e.memset(h2T[:, :, 0:1], 0.0), writes=[B_h2T])
                    lo, c_lo = 0, 1
                if hi > NQ:
                    assert hi - 1 == S, "right halo missing"
                    P.op("dve", lambda e: e.memset(h2T[:, :, W_ - 1:W_], 0.0), writes=[B_h2T])
                    hi, c_hi = hi - 1, W_ - 1
                P.dma("sp", h2T[:, :, c_lo:c_hi], j["H2T"][:, :, lo:hi].rearrange("k p n -> p k n"), writes=[B_h2T])

            def epi4(blk, s_, nsz):
                j, t0, nt = blk
                x1s, B_x1s = R["x1"].next()
                r0 = t0 + s_ * 128
                P.dma_multi("sp", [(x1s[0:nsz, c_:c_ + 512], j["X1"][r0:r0 + nsz, c_:c_ + 512]) for c_ in range(0, D, 512)], writes=[B_x1s])
                resid_norm(R, Rt, B_Rt, s_, x1s[0:nsz, :], B_x1s, g3[0:nsz, :], B_g3, nsz)
                P.dma_multi("sp", [(j["y"][r0:r0 + nsz, c_:c_ + 512], Rt[0:nsz, s_, c_:c_ + 512]) for c_ in range(0, D, 512)], reads=[B_Rt])

            load_h2T(blocks4[0])
            cur_job = None
            for bi, blk in enumerate(blocks4):
                j, t0, nt = blk
                if j is not cur_job:
                    P.dma("sp", cw[:], j["convw"][:, :, :], writes=[B_cw])
                    cur_job = j
                prev = blocks4[bi - 1] if bi > 0 else None
                nxt = blocks4[bi + 1] if bi + 1 < len(blocks4) else None
                W_ = nt + 2
                esched = {}
                if prev is not None:
                    for k_, (s_, nsz) in enumerate(subtiles(prev[2])):
                        esched[1 + 3 * k_] = (s_, nsz)
                pipe = Pipe()
                for i in range(NFC):
                    if i in esched:
                        epi4(prev, *esched[i])
                    if i % 2 == 0:
                        wu, B_wu = R["w"].next()
                        P.dma_multi("pool", [(wu[:, :, 0:256], w_up_v[:, :, i * 128:i * 128 + 256]),
                                             (wu[:, :, 256:512], w_up_v[:, :, DFF + i * 128:DFF + i * 128 + 256])], writes=[B_wu])
                    bks = []
                    for half in range(2):
                        c0_ = half * 256 + (i % 2) * 128
                        cs = slice(c0_, c0_ + 128)
                        Wm = min(W_, 512)
                        bm, B_bm = R["pf"].next()
                        mm_group(bm[:, 0:Wm], [(wu[:, kc, cs], h2T[:, kc, 0:Wm]) for kc in range(NKC)], [B_wu, B_h2T], [B_bm])
                        bt, B_bt = None, None
                        if W_ > 512:
                            bt, B_bt = R["pf"].next()
                            mm_group(bt[:, 0:W_ - 512], [(wu[:, kc, cs], h2T[:, kc, 512:W_]) for kc in range(NKC)], [B_wu, B_h2T], [B_bt])
                        bks.append((bm, B_bm, bt, B_bt))

                    def post(i=i, bks=bks, nt=nt, W_=W_):
                        cv = []
                        for half in range(2):
                            bm, B_bm, bt, B_bt = bks[half]
                            ci = i + half * NFC
                            u, B_u = R["ub"].next()
                            Wm = min(W_, 512)
                            P.op("act", lambda e: e.activation(out=u[:, 0:Wm], in_=bm[:, 0:Wm], func=AF.Copy), reads=[B_bm], writes=[B_u])
                            if bt is not None:
                                P.op("act", lambda e: e.activation(out=u[:, 512:W_], in_=bt[:, 0:W_ - 512], func=AF.Copy), reads=[B_bt], writes=[B_u])
                            a_, B_a = R["tf"].next()
                            P.op("dve", lambda e: e.tensor_scalar(out=a_[:, 0:nt], in0=u[:, 0:nt], scalar1=cw[:, 0, ci:ci + 1], scalar2=convb[:, ci:ci + 1],
                                                                  op0=ALU.mult, op1=ALU.add),
                                 reads=[B_u, B_cw, B_convb], writes=[B_a])
                            P.op("dve", lambda e: e.scalar_tensor_tensor(out=a_[:, 0:nt], in0=u[:, 1:nt + 1], scalar=cw[:, 1, ci:ci + 1], in1=a_[:, 0:nt],
                                                                         op0=ALU.mult, op1=ALU.add),
                                 reads=[B_u, B_cw, B_a], writes=[B_a])
                            P.op("dve", lambda e: e.scalar_tensor_tensor(out=a_[:, 0:nt], in0=u[:, 2:nt + 2], scalar=cw[:, 2, ci:ci + 1], in1=a_[:, 0:nt],
                                                                         op0=ALU.mult, op1=ALU.add),
                                 reads=[B_u, B_cw, B_a], writes=[B_a])
                            cv.append((a_, B_a))
                        (a_, B_a), (b_, B_b) = cv
                        P.op("act", lambda e: e.activation(out=a_[:, 0:nt], in_=a_[:, 0:nt], func=AF.Gelu_apprx_tanh), reads=[B_a], writes=[B_a])
                        P.op("dve", lambda e: e.tensor_tensor(out=gT[:, i, 0:nt], in0=a_[:, 0:nt], in1=b_[:, 0:nt], op=ALU.mult),
                             reads=[B_a, B_b], writes=[B_gT])
                    pipe.push(post)
                pipe.flush()
                KH = NFC // 2
                for cp in range(NKC // 2):
                    if cp == 0 and nxt is not None:
                        load_h2T(nxt)
                    bks2 = [R["pf"].next(), R["pf"].next()]
                    for kh in range(2):
                        wd, B_wd = R["wd"].next()
                        P.dma("pool", wd[:], w_down_v[:, kh * KH:(kh + 1) * KH, cp * 256:(cp + 1) * 256], writes=[B_wd])
                        for half in range(2):
                            bk, B_bk = bks2[half]
                            for kk in range(KH):
                                kc = kh * KH + kk
                                mm(bk[:, 0:nt], wd[:, kk, half * 128:(half + 1) * 128], gT[:, kc, 0:nt], kc == 0, kc == NFC - 1,
                                   [B_wd, B_gT], [B_bk], kc == NFC - 1)
                    for half in range(2):
                        bk, B_bk = bks2[half]
                        pipe.push(lambda c=2 * cp + half, bk=bk, B_bk=B_bk, nt=nt: to_token_major(R, bk, B_bk, Rt, B_Rt, c, nt))
                pipe.flush()
            last = blocks4[-1]
            for (s_, nsz) in subtiles(last[2]):
                epi4(last, s_, nsz)
            P.barrier()
            P.emit()
        nc._prog_stats = (P.n_ops, P.n_wait, dict(P.cnt), max(P.dma_cnt))
    return nc


def _rope_tables(S, reverse):
    t = np.arange(S)
    row = (t // GRID_W).astype(np.float32)
    col = (t % GRID_W).astype(np.float32)
    tf = t.astype(np.float32)
    inv64 = (np.float32(THETA) ** (-(np.arange(0, 64, 2, dtype=np.float32)) / np.float32(64))).astype(np.float32)
    inv128 = (np.float32(THETA) ** (-(np.arange(0, 128, 2, dtype=np.float32)) / np.float32(128))).astype(np.float32)
    angA = np.zeros((128, S), np.float32)
    for d in range(128):
        pos = row if d < 64 else col
        angA[d] = pos * inv64[(d % 64) % 32]
    angB = np.zeros((128, S), np.float32)
    for d in range(128):
        angB[d] = tf * inv128[d % 64]
    tab = np.stack([np.cos(angA.astype(np.float64)), np.sin(angA.astype(np.float64)),
                    np.cos(angB.astype(np.float64)), np.sin(angB.astype(np.float64))]).astype(np.float32)
    if reverse:
        tab = tab[:, :, ::-1]
    return np.ascontiguousarray(tab)


def _consts():
    bf = ml_dtypes.bfloat16
    identb = np.eye(128, dtype=np.float32)
    onesb = np.ones((128, 128), np.float32)
    rotA = np.zeros((128, 128), np.float32)
    for base in (0, 64):
        for m in range(base, base + 32):
            rotA[m + 32, m] = -1.0
        for m in range(base + 32, base + 64):
            rotA[m - 32, m] = 1.0
    rotB = np.zeros((128, 128), np.float32)
    for m in range(64):
        rotB[m + 64, m] = -1.0
    for m in range(64, 128):
        rotB[m - 64, m] = 1.0
    kk = np.arange(128)[:, None]
    qq = np.arange(128)[None, :]
    mP = np.tile((qq <= kk).astype(np.float32), (1, 4))
    mN = np.tile((kk <= qq).astype(np.float32), (1, 4))
    cbf = np.concatenate([identb, onesb, rotA, rotB, mP, mN], axis=1).astype(bf)
    return np.eye(128, dtype=np.float32), np.ascontiguousarray(cbf)


def _convw_layout(cw, reverse):
    if reverse:
        cw = cw[::-1]
    return np.ascontiguousarray(cw.reshape(3, 88, 128).transpose(2, 0, 1))


_CACHE = {}


def _get_program(jobs_key):
    if jobs_key not in _CACHE:
        jobs = [dict(name=n, S=S, NQ=NQ, NOUT=NOUT) for (n, S, NQ, NOUT) in jobs_key]
        _CACHE[jobs_key] = build_program(jobs)
    return _CACHE[jobs_key]


def run_cores(core_inputs, jobs_key):
    raise NotImplementedError


def kernel(x_prompt, x_sample, norm_pre_mix, w_in, q_norm_a, k_norm_a, sink_b, w_branch_a,
           w_branch_b, w_out, norm_post_mix, norm_pre_ffn, w_up, conv_w, conv_b, w_down,
           norm_post_ffn, _n_cores=8):
    f = lambda a: np.ascontiguousarray(np.asarray(a, dtype=np.float32))
    x_prompt = f(x_prompt); x_sample = f(x_sample)
    Bp, Sp, _ = x_prompt.shape
    Bs, Ss, _ = x_sample.shape
    half = Sp // 2
    jobs_key = (("p", Sp, half + 128, half), ("s", Ss, Ss, Ss))
    nc = _get_program(jobs_key)

    identf, cbf = _consts()
    gbc = np.ascontiguousarray(np.stack([np.broadcast_to(f(g)[0][None, :], (128, D))
                                         for g in (norm_pre_mix, norm_post_mix, norm_pre_ffn, norm_post_ffn)]))
    qkn = np.ascontiguousarray(np.stack([f(q_norm_a)[0], f(k_norm_a)[0]], axis=1))
    sinkbc = np.ascontiguousarray(np.broadcast_to(f(sink_b)[0][None, :], (128, 8)))
    convb = np.ascontiguousarray(f(conv_b)[0].reshape(88, 128).T)
    cw = f(conv_w)[0]
    shared = dict(w_in=f(w_in)[0], w_ba=f(w_branch_a)[0], w_bb=f(w_branch_b)[0], w_out=f(w_out)[0],
                  w_up=f(w_up)[0], w_down=f(w_down)[0], gbc=gbc, qkn=qkn, sinkbc=sinkbc, convb=convb,
                  identf=identf, cbf=cbf)
    tab_p = {False: _rope_tables(Sp, False), True: _rope_tables(Sp, True)}
    tab_s = _rope_tables(Ss, False)
    cw_l = {False: _convw_layout(cw, False), True: _convw_layout(cw, True)}

    n_cores = _n_cores
    in_maps = []
    for c in range(n_cores):
        p, hf = c // 2, c % 2
        rev = hf == 1
        xp = x_prompt[p % Bp]
        if rev:
            xp = np.ascontiguousarray(xp[::-1])
        m = dict(shared)
        m["x_p"] = xp
        m["tab_p"] = tab_p[rev]
        m["convw_p"] = cw_l[rev]
        m["x_s"] = x_sample[c % Bs]
        m["tab_s"] = tab_s
        m["convw_s"] = cw_l[False]
        in_maps.append(m)
    res = run_bass_kernel_spmd(nc, in_maps, core_ids=list(range(n_cores)))
    y_prompt = np.zeros((Bp, Sp, D), np.float32)
    y_sample = np.zeros((Bs, Ss, D), np.float32)
    for c in range(n_cores):
        p, hf = c // 2, c % 2
        r = res.results[c]
        yp = np.asarray(r["y_p"], dtype=np.float32)
        if hf == 0:
            y_prompt[p % Bp, 0:half] = yp
        else:
            y_prompt[p % Bp, half:] = yp[::-1]
        y_sample[c % Bs] = np.asarray(r["y_s"], dtype=np.float32)
    return (y_prompt, y_sample)
```
